# Optimizing a Trainium2 kernel written in Bass

```python
import jax, jax.numpy as jnp
from jax import lax
import numpy as np

D_MODEL = 1024
BATCH = 8
SEQ = 2048
DEPTH = 1

D_CONV = 1024
CONV_W = 3
N_HEADS = 16
HEAD_DIM = 64
N_KV = 4
GROUP = N_HEADS // N_KV
D_ATT = N_HEADS * HEAD_DIM
D_KV = N_KV * HEAD_DIM
CMP_LEN = 32
CMP_STRIDE = 16
CMP_HID = 128
SLC_LEN = 64
N_SEL = 8
WINDOW = 512
Q_BLOCK = 64
N_MIXERS = 2
NORM_EPS = 1e-6

SPLITS = (D_CONV, D_CONV, D_CONV, D_CONV,
          D_ATT,
          D_KV, D_KV, D_KV, D_KV, D_KV, D_KV,
          3 * N_HEADS,
          D_ATT,
          N_MIXERS * D_MODEL)
N_IN = 4 * D_CONV + 2 * D_ATT + 6 * D_KV + 3 * N_HEADS + N_MIXERS * D_MODEL

kernel_name = "hybrid_shortconv_nsa_gated_merge"


def rmsnorm(x, w):
    xf = x.astype(jnp.float32)
    r = lax.rsqrt(jnp.mean(xf * xf, axis=-1, keepdims=True) + NORM_EPS)
    return (xf * r).astype(x.dtype) * w


def alibi_slopes():
    return jnp.exp2(-8.0 * jnp.arange(1, N_HEADS + 1, dtype=jnp.float32) / N_HEADS)


def masked_softmax(s, mask):
    s = jnp.where(mask, s.astype(jnp.float32), -jnp.inf)
    m = jnp.max(s, axis=-1, keepdims=True)
    m = jnp.where(jnp.isfinite(m), m, 0.0)
    e = jnp.where(mask, jnp.exp(s - m), 0.0)
    return e / jnp.maximum(jnp.sum(e, axis=-1, keepdims=True), 1e-30)


def short_conv_mixer(h, b_gate, c_gate, z, conv_w, conv_b):
    u = c_gate * h
    T = u.shape[1]
    up = jnp.pad(u, ((0, 0), (CONV_W - 1, 0), (0, 0)))
    y = conv_b + conv_w[0] * up[:, 0:T] + conv_w[1] * up[:, 1:T + 1] + conv_w[2] * up[:, 2:T + 2]
    return b_gate * y * jax.nn.silu(z)


def compress_blocks(k, pe, w1, w2):
    B, G, T, dh = k.shape
    kd = k.reshape(B, G, T // CMP_STRIDE, CMP_STRIDE, dh)
    blocks = jnp.concatenate([kd[:, :, :-1], kd[:, :, 1:]], axis=3)
    blocks = (blocks + pe).reshape(B, G, blocks.shape[2], CMP_LEN * dh)
    return jax.nn.silu(blocks @ w1) @ w2


def nsa_attention(q, k_cmp, v_cmp, k_slc, v_slc, k_win, v_win, gates,
                  pe_k, pe_v, w1_k, w2_k, w1_v, w2_v):
    B, T = q.shape[0], q.shape[1]
    f32 = jnp.float32
    q = (q * HEAD_DIM ** -0.5).reshape(B, T, N_KV, GROUP, HEAD_DIM).transpose(0, 2, 3, 1, 4)
    tr = lambda a: a.transpose(0, 2, 1, 3)
    slopes = alibi_slopes().reshape(N_KV, GROUP)[None, :, :, None, None]
    t_pos = jnp.arange(T)

    kc = compress_blocks(tr(k_cmp), pe_k, w1_k, w2_k)
    vc = compress_blocks(tr(v_cmp), pe_v, w1_v, w2_v)
    n_cmp = kc.shape[2]
    c_start = jnp.arange(n_cmp) * CMP_STRIDE
    c_end = c_start + CMP_LEN - 1
    dist_c = t_pos[:, None] - c_end[None, :]
    s_c = jnp.einsum('bgrtd,bgcd->bgrtc', q, kc) - slopes * dist_c.astype(f32)
    p_cmp = masked_softmax(s_c, dist_c >= 0)
    o_cmp = jnp.einsum('bgrtc,bgcd->bgrtd', p_cmp.astype(vc.dtype), vc)

    n_slc = T // SLC_LEN
    s_start = jnp.arange(n_slc) * SLC_LEN
    overlap = ((c_start[:, None] <= s_start[None, :] + SLC_LEN - 1)
               & (c_end[:, None] >= s_start[None, :])).astype(f32)
    imp = jnp.einsum('bgrtc,cj->bgtj', p_cmp, overlap)
    cur = (t_pos // SLC_LEN)[:, None]
    j = jnp.arange(n_slc)[None, :]
    forced = (j == 0) | (j == cur) | (j == cur - 1)
    imp = jnp.where(forced, jnp.inf, jnp.where(j <= cur, imp, -jnp.inf))
    n_top = min(N_SEL, n_slc)
    top_val, top_idx = lax.top_k(imp, n_top)
    top_ok = top_val > -jnp.inf

    ks_blocks = tr(k_slc).reshape(B, N_KV, n_slc, SLC_LEN, HEAD_DIM)
    vs_blocks = tr(v_slc).reshape(B, N_KV, n_slc, SLC_LEN, HEAD_DIM)
    kw = jnp.pad(tr(k_win), ((0, 0), (0, 0), (WINDOW, 0), (0, 0)))
    vw = jnp.pad(tr(v_win), ((0, 0), (0, 0), (WINDOW, 0), (0, 0)))
    gather = jax.vmap(jax.vmap(lambda blk, ix: blk[ix]))
    n_sel_keys = n_top * SLC_LEN

    def block_fn(i):
        t0 = i * Q_BLOCK
        qb = lax.dynamic_slice_in_dim(q, t0, Q_BLOCK, axis=3)
        tq = t0 + jnp.arange(Q_BLOCK)
        ix = lax.dynamic_slice_in_dim(top_idx, t0, Q_BLOCK, axis=2)
        ok = lax.dynamic_slice_in_dim(top_ok, t0, Q_BLOCK, axis=2)
        ks = gather(ks_blocks, ix).reshape(B, N_KV, Q_BLOCK, n_sel_keys, HEAD_DIM)
        vs = gather(vs_blocks, ix).reshape(B, N_KV, Q_BLOCK, n_sel_keys, HEAD_DIM)
        s_pos = (ix[..., None] * SLC_LEN + jnp.arange(SLC_LEN)).reshape(B, N_KV, Q_BLOCK, n_sel_keys)
        s_ok = jnp.repeat(ok, SLC_LEN, axis=-1)
        d_s = tq[:, None] - s_pos
        s_s = jnp.einsum('bgrqd,bgqkd->bgrqk', qb, ks) - slopes * d_s[:, :, None].astype(f32)
        p_s = masked_softmax(s_s, (s_ok & (d_s >= 0))[:, :, None])
        o_s = jnp.einsum('bgrqk,bgqkd->bgrqd', p_s.astype(vs.dtype), vs)
        kwb = lax.dynamic_slice_in_dim(kw, t0, Q_BLOCK + WINDOW, axis=2)
        vwb = lax.dynamic_slice_in_dim(vw, t0, Q_BLOCK + WINDOW, axis=2)
        w_pos = t0 - WINDOW + jnp.arange(Q_BLOCK + WINDOW)
        d_w = tq[:, None] - w_pos[None, :]
        s_w = jnp.einsum('bgrqd,bgkd->bgrqk', qb, kwb) - slopes * d_w.astype(f32)
        p_w = masked_softmax(s_w, (d_w >= 0) & (d_w < WINDOW) & (w_pos[None, :] >= 0))
        o_w = jnp.einsum('bgrqk,bgkd->bgrqd', p_w.astype(vwb.dtype), vwb)
        return o_s, o_w

    o_slc, o_win = lax.map(block_fn, jnp.arange(T // Q_BLOCK))
    unblock = lambda o: o.transpose(1, 2, 3, 0, 4, 5).reshape(B, N_KV, GROUP, T, HEAD_DIM)
    o = jnp.stack([o_cmp, unblock(o_slc), unblock(o_win)], axis=-1)
    o = o.transpose(0, 3, 1, 2, 4, 5).reshape(B, T, N_HEADS, HEAD_DIM, 3)
    return jnp.einsum('bthdc,bthc->bthd', o, gates).reshape(B, T, D_ATT)


def setup_inputs(seed: int = 0) -> dict:
    key = jax.random.key(seed)
    ks = jax.random.split(key, 16)
    nrm = lambda k, shape, scale: jax.random.normal(k, shape, jnp.float32) * scale
    return {
        "x": nrm(ks[0], (BATCH, SEQ, D_MODEL), 1.0),
        "norm_w": 1.0 + nrm(ks[1], (DEPTH, D_MODEL), 0.02),
        "w_in": nrm(ks[2], (DEPTH, D_MODEL, N_IN), D_MODEL ** -0.5),
        "conv_w": nrm(ks[3], (DEPTH, CONV_W, D_CONV), CONV_W ** -0.5),
        "conv_b": nrm(ks[4], (DEPTH, D_CONV), 0.02),
        "cmp_pe_k": nrm(ks[5], (DEPTH, CMP_LEN, HEAD_DIM), 0.1),
        "cmp_pe_v": nrm(ks[6], (DEPTH, CMP_LEN, HEAD_DIM), 0.1),
        "cmp_w1_k": nrm(ks[7], (DEPTH, CMP_LEN * HEAD_DIM, CMP_HID), (CMP_LEN * HEAD_DIM) ** -0.5),
        "cmp_w2_k": nrm(ks[8], (DEPTH, CMP_HID, HEAD_DIM), CMP_HID ** -0.5),
        "cmp_w1_v": nrm(ks[9], (DEPTH, CMP_LEN * HEAD_DIM, CMP_HID), (CMP_LEN * HEAD_DIM) ** -0.5),
        "cmp_w2_v": nrm(ks[10], (DEPTH, CMP_HID, HEAD_DIM), CMP_HID ** -0.5),
        "w_proj_a": nrm(ks[11], (DEPTH, D_CONV, D_MODEL), D_CONV ** -0.5),
        "w_proj_b": nrm(ks[12], (DEPTH, D_ATT, D_MODEL), D_ATT ** -0.5),
        "w_out": nrm(ks[13], (DEPTH, D_MODEL, D_MODEL), D_MODEL ** -0.5),
        "final_norm_w": 1.0 + nrm(ks[14], (D_MODEL,), 0.02),
    }


def reference(x, norm_w, w_in, conv_w, conv_b, cmp_pe_k, cmp_pe_v, cmp_w1_k, cmp_w2_k,
              cmp_w1_v, cmp_w2_v, w_proj_a, w_proj_b, w_out, final_norm_w):
    B, T, _ = x.shape
    offsets = np.cumsum(SPLITS)[:-1].tolist()
    for l in range(DEPTH):
        h = rmsnorm(x, norm_w[l])
        proj = h @ w_in[l]
        (h_a, b_a, c_a, z_a, q, k_c, v_c, k_s, v_s, k_w, v_w,
         g_nsa, z_b, g_mix) = jnp.split(proj, offsets, axis=-1)
        y_a = short_conv_mixer(h_a, b_a, c_a, z_a, conv_w[l], conv_b[l]) @ w_proj_a[l]
        kv = lambda a: a.reshape(B, T, N_KV, HEAD_DIM)
        o_b = nsa_attention(q.reshape(B, T, N_HEADS, HEAD_DIM), kv(k_c), kv(v_c), kv(k_s), kv(v_s),
                            kv(k_w), kv(v_w), jax.nn.sigmoid(g_nsa).reshape(B, T, N_HEADS, 3),
                            cmp_pe_k[l], cmp_pe_v[l], cmp_w1_k[l], cmp_w2_k[l], cmp_w1_v[l], cmp_w2_v[l])
        y_b = (o_b * jax.nn.silu(z_b)) @ w_proj_b[l]
        g = jax.nn.sigmoid(g_mix).reshape(B, T, N_MIXERS, D_MODEL)
        mixed = g[:, :, 0] * y_a + g[:, :, 1] * y_b
        x = x + mixed @ w_out[l]
    return rmsnorm(x, final_norm_w)
```

```python
from contextlib import ExitStack

import ml_dtypes
import numpy as np
import concourse.bass as bass
import concourse.mybir as mybir
from concourse.bass_utils import run_bass_kernel_spmd

F32 = mybir.dt.float32
BF16 = mybir.dt.bfloat16
AF = mybir.ActivationFunctionType
ALU = mybir.AluOpType


class _Buf:
    __slots__ = ("name", "w", "r")

    def __init__(self, name):
        self.name = name
        self.w = None
        self.r = {}


class _Eng:
    def __init__(self, name, handle, sem):
        self.name, self.h, self.sem = name, handle, sem
        self.cnt = 0
        self.known = {}


class Sched:
    def __init__(self, nc, n_dma_sems=48):
        self.nc = nc
        hs = {"pe": nc.tensor, "act": nc.scalar, "dve": nc.vector, "pool": nc.gpsimd, "sp": nc.sync}
        self.eng = {k: _Eng(k, h, nc.alloc_semaphore("sem_" + k)) for k, h in hs.items()}
        self.dsem = [[nc.alloc_semaphore("dsem%d" % i), 0] for i in range(n_dma_sems)]
        n3 = n_dma_sems // 3
        self.dpool = {"sp": self.dsem[:n3], "act": self.dsem[n3:2 * n3], "pool": self.dsem[2 * n3:]}
        self.dnext = {"sp": 0, "act": 0, "pool": 0}
        self.bufs = {}
        self.out_toks = []

    def buf(self, name):
        b = self.bufs.get(name)
        if b is None:
            b = self.bufs[name] = _Buf(name)
        return b

    def _deps(self, reads, writes):
        deps = {}

        def add(t):
            k = id(t[0])
            if k not in deps or deps[k][1] < t[1]:
                deps[k] = t
        for b in reads:
            if b.w is not None:
                add(b.w)
        for b in writes:
            if b.w is not None:
                add(b.w)
            for t in b.r.values():
                add(t)
        return deps

    def _wait(self, e, deps, skip_own=False):
        for k, (sem, val) in deps.items():
            if skip_own and sem is e.sem:
                continue
            if e.known.get(k, 0) >= val:
                continue
            e.h.wait_ge(sem, val)
            e.known[k] = val

    def _mark(self, tok, reads, writes):
        k = id(tok[0])
        for b in reads:
            b.r[k] = tok
        for b in writes:
            b.w = tok
            b.r = {}

    def op(self, en, fn, reads=(), writes=()):
        e = self.eng[en]
        xr = [b for b in reads if b.name.startswith("ps")]
        if xr:
            reads = [b for b in reads if not b.name.startswith("ps")]
            writes = list(writes) + xr
        self._wait(e, self._deps(reads, writes))
        e.cnt += 1
        fn(e.h).then_inc(e.sem, 1)
        tok = (e.sem, e.cnt)
        self._mark(tok, reads, writes)
        return tok

    def mm(self, fns, reads=(), writes=()):
        e = self.eng["pe"]
        self._wait(e, self._deps(reads, writes), skip_own=True)
        for f in fns[:-1]:
            f(e.h)
        e.cnt += 1
        fns[-1](e.h).then_inc(e.sem, 1)
        tok = (e.sem, e.cnt)
        self._mark(tok, reads, writes)
        return tok

    def dma(self, en, out, in_, reads=(), writes=(), is_out=False, **kw):
        e = self.eng[en]
        self._wait(e, self._deps(reads, writes))
        pool = self.dpool[en]
        slot = pool[self.dnext[en]]
        self.dnext[en] = (self.dnext[en] + 1) % len(pool)
        if slot[1] and e.known.get(id(slot[0]), 0) < slot[1]:
            e.h.wait_ge(slot[0], slot[1])
            e.known[id(slot[0])] = slot[1]
        slot[1] += 16
        e.h.dma_start(out=out, in_=in_, **kw).then_inc(slot[0], 16)
        tok = (slot[0], slot[1])
        self._mark(tok, reads, writes)
        if is_out:
            self.out_toks.append(tok)
        return tok

    def barrier(self):
        toks = {}

        def add(t):
            k = id(t[0])
            if k not in toks or toks[k][1] < t[1]:
                toks[k] = t
        for e in self.eng.values():
            if e.cnt:
                add((e.sem, e.cnt))
        keep = {}
        for n, b in self.bufs.items():
            if n.startswith("wblk"):
                keep[n] = b
                continue
            if b.w is not None:
                add(b.w)
            for t in b.r.values():
                add(t)
        for e in self.eng.values():
            for k, (sem, val) in toks.items():
                if e.known.get(k, 0) < val:
                    e.h.wait_ge(sem, val)
                    e.known[k] = val
        self.bufs = keep

    def finish(self):
        e = self.eng["sp"]
        for sem, val in self.out_toks:
            if e.known.get(id(sem), 0) < val:
                e.h.wait_ge(sem, val)
                e.known[id(sem)] = val


D = 1024
T = 2048
NH = 16
NG = 4
HD = 64
N_IN = 9776
NEG = -30000.0
EPS = 1e-6
O_HA, O_B, O_C, O_ZA, O_Q = 0, 1024, 2048, 3072, 4096
O_KC, O_VC, O_KS, O_VS, O_KW, O_VW = 5120, 5376, 5632, 5888, 6144, 6400
O_GN, O_ZB, O_GM = 6656, 6704, 7728
NCMP = 127
DEBUG = False


def _bf16r(x):
    u = np.ascontiguousarray(x, dtype=np.float32).view(np.uint32).astype(np.uint64)
    r = ((u + 0x7FFF + ((u >> 16) & 1)) >> 16) << 16
    return (r & 0xFFFFFFFF).astype(np.uint32).view(np.float32)


def _consts():
    c = {}
    c["ident"] = np.eye(128, dtype=np.float32)
    k = np.arange(128)[:, None]
    t = np.arange(128)[None, :]
    c["tri_up"] = np.where(k <= t, 0.0, NEG).astype(np.float32)
    c["tri_lo"] = np.where(k > t, 0.0, NEG).astype(np.float32)
    cend = 16 * np.arange(128) + 31
    tt = np.arange(T)
    cm = np.where(cend[:, None] <= tt[None, :], 0.0, NEG).astype(np.float32)
    c["cmask"] = cm
    key = np.arange(T)
    alk = np.stack([np.ones(T), np.ones(T), 64.0 * (key // 64), 64.0 * (key // 64), key % 64, key % 64, 64.0 * (key // 64), key % 64]).astype(np.float32)
    E = (key[None, :] // 64 == np.arange(32)[:, None]).astype(np.float32)
    c["kaug_s"] = np.concatenate([E, alk], 0)
    c["kaug_w"] = np.concatenate([np.zeros((32, T), np.float32), alk], 0)
    pc = cend[:NCMP]
    alc = np.stack([np.ones(NCMP), np.ones(NCMP), 64.0 * (pc // 64), 64.0 * (pc // 64), pc % 64, pc % 64, 64.0 * (pc // 64), pc % 64]).astype(np.float32)
    kc = np.concatenate([np.zeros((32, NCMP), np.float32), alc], 0)
    c["kaug_c"] = np.tile(kc, (1, NG))
    slopes = np.exp2(-8.0 * np.arange(1, NH + 1, dtype=np.float64) / NH).astype(np.float32)
    qal = np.zeros((NH, 8, T), np.float32)
    for h in range(NH):
        a = (-slopes[h].astype(np.float64) * tt).astype(np.float32)
        ahi = _bf16r(a)
        alo = _bf16r(a - ahi)
        shi = _bf16r(slopes[h:h + 1])[0]
        slo = _bf16r(np.float32(slopes[h] - shi).reshape(1))[0]
        slo2 = _bf16r(np.float32(np.float64(slopes[h]) - np.float64(shi) - np.float64(slo)).reshape(1))[0]
        qal[h, 0], qal[h, 1] = ahi, alo
        qal[h, 2], qal[h, 3], qal[h, 4], qal[h, 5] = shi, slo, shi, slo
        qal[h, 6], qal[h, 7] = slo2, slo2
    c["qal"] = qal.astype(ml_dtypes.bfloat16)
    cs = 16 * np.arange(NCMP)
    ce = cs + 31
    ss = 64 * np.arange(32)
    ov = ((cs[:, None] <= ss[None, :] + 63) & (ce[:, None] >= ss[None, :])).astype(np.float32)
    vcc = np.concatenate([np.ones((NCMP, 1), np.float32), ov], 1)
    c["vcc"] = np.tile(vcc[:, None, :], (1, NG, 1))
    cur = (tt // 64)[:, None]
    j = np.arange(32)[None, :]
    forced = (j == 0) | (j == cur) | (j == cur - 1)
    F = np.where(forced, 100.0, np.where(j <= cur, 0.0, -100.0)).astype(np.float32)
    c["fimp"] = np.ascontiguousarray(F.reshape(16, 128, 32).transpose(1, 0, 2))
    return c


_CONST_SHAPES = {"ident": [128, 128], "tri_up": [128, 128], "tri_lo": [128, 128], "cmask": [128, T],
                 "kaug_s": [40, T], "kaug_w": [40, T], "kaug_c": [40, NG * NCMP], "qal": [NH, 8, T],
                 "vcc": [NCMP, NG, 33], "fimp": [128, 16, 32]}

_IN_SHAPES = {"x": [T, D], "w_in": [D, N_IN], "nw_b": [128, D], "conv_wT": [128, 8, 3], "conv_bT": [128, 8],
              "peT_k": [64, 64], "peT_v": [64, 64], "w1k": [2048, 128], "w1v": [2048, 128],
              "w2k": [128, 64], "w2v": [128, 64], "wpa": [D, D], "wpb": [D, D], "wo": [D, D], "fnw_b": [128, D]}


def build_nc(debug=False, stages=99):
    nc = bass.Bass("TRN2", target_bir_lowering=False)
    dr = {}
    for n, s in list(_IN_SHAPES.items()) + list(_CONST_SHAPES.items()):
        dr[n] = nc.dram_tensor(n, s, BF16 if n == "qal" else F32, kind="ExternalInput").ap()
    out_d = nc.dram_tensor("out", [T, D], F32, kind="ExternalOutput").ap()
    dbg = {}
    S = Sched(nc)
    B = S.buf
    sb = lambda n, shp, dt: nc.alloc_sbuf_tensor('s_' + n, shp, dt)
    PS = [nc.alloc_psum_tensor("psb%d" % i, [128, 512], F32) for i in range(8)]
    PB = ["ps%d" % i for i in range(8)]

    def psbf(i):
        return PS[i][:, :].bitcast(BF16)

    hT = sb("hT", [128, 8, T], BF16)
    obT = sb("obT", [128, 8, T], BF16)
    NWB = 8
    wpool = [sb("wblk%d" % i, [128, 8, 256], BF16) for i in range(NWB)]
    ident = sb("ident", [128, 128], BF16)
    tri_up = sb("tri_up", [128, 128], BF16)
    tri_lo = sb("tri_lo", [128, 128], BF16)
    gates = sb("gates", [128, 16, 48], F32)
    KC = sb("KC", [128, NG * NCMP], BF16)
    VC = sb("VC", [NCMP, NG, 97], BF16)
    cwT = sb("cwT", [128, 8, 3], F32)
    cbT = sb("cbT", [128, 8], F32)
    stat = sb("stat", [128, 64], F32)

    win_v = dr["w_in"].rearrange("(kt p) c -> p kt c", p=128)

    def wv_(n):
        return dr[n].rearrange("(kt p) c -> p kt c", p=128)

    wplan = []
    wplan.append(([(O_GN, 48)], None))
    for g in range(NG):
        wplan.append(([(O_KC + 64 * g, 64), (O_VC + 64 * g, 64)], None))
    for g in range(NG):
        wplan.append(([(O_Q + 256 * g, 256)], None))
        wplan.append(([(O_KS + 64 * g, 64), (O_KW + 64 * g, 64)], None))
        wplan.append(([(O_ZB + 256 * g, 256)], None))
        wplan.append(([(O_VS + 64 * g, 64), (O_VW + 64 * g, 64)], None))
    for cp in range(4):
        for off in (O_HA, O_B, O_C, O_ZA):
            wplan.append(([(off + 256 * cp, 256)], None))
    for cp in range(4):
        wplan.append(([(256 * cp, 256)], "wpa"))
        wplan.append(([(256 * cp, 256)], "wpb"))
        wplan.append(([(O_GM + 256 * cp, 256)], None))
        wplan.append(([(O_GM + D + 256 * cp, 256)], None))
    for j in range(4):
        wplan.append(([(256 * j, 256)], "wo"))
    wstate = {"next": 0, "issued": 0}
    AHEAD = 3

    def _wissue(upto):
        while wstate["issued"] < min(upto, len(wplan)):
            j = wstate["issued"]
            wstate["issued"] += 1
            segs, src = wplan[j]
            blk, bb = wpool[j % NWB], B("wblk%d" % (j % NWB))
            v = win_v if src is None else wv_(src)
            o = 0
            for (c0, n) in segs:
                S.dma("pool", blk[:, :, o:o + n], v[:, :, c0:c0 + n], writes=[bb])
                o += n

    def wprefetch(n):
        _wissue(wstate["next"] + n)

    def wblock(segs, src=None):
        j = wstate["next"]
        wstate["next"] += 1
        psegs, psrc = wplan[j]
        assert psegs == list(segs) and (psrc == src), (j, psegs, segs, psrc, src)
        _wissue(j + 1 + AHEAD)
        return wpool[j % NWB], B("wblk%d" % (j % NWB))
    wblock.prefetch = wprefetch

    def cload(dst, src, name, cast):
        S.dma("pool" if cast else "sp", dst, src, writes=[B(name)])

    cload(ident[:, :], dr["ident"], "ident", True)
    cload(tri_up[:, :], dr["tri_up"], "tri_up", True)
    cload(tri_lo[:, :], dr["tri_lo"], "tri_lo", True)
    cload(cwT[:, :, :], dr["conv_wT"], "cwT", False)
    cload(cbT[:, :], dr["conv_bT"], "cbT", False)

    def proj_fm(ps_ap, psname, blk, bb, c0, m, tg, extra_reads=()):
        fns = []
        for k in range(8):
            fns.append(lambda e, k=k: e.matmul(ps_ap, blk[:, k, c0:c0 + m], hT[:, k, tg * 512:(tg + 1) * 512],
                                               start=(k == 0), stop=(k == 7)))
        S.mm(fns, reads=[bb, B("hT%d" % tg)] + list(extra_reads), writes=[B(psname)])

    xv = dr["x"].rearrange("(tt p) d -> tt p d", p=128)
    with ExitStack() as _es:
        xts = [_es.enter_context(nc.sbuf_tensor("s_xt%d" % i, [128, D], F32)) for i in range(4)]
        xns = [_es.enter_context(nc.sbuf_tensor("s_xn%d" % i, [128, D], BF16)) for i in range(4)]
        nwb = _es.enter_context(nc.sbuf_tensor("s_nwb", [128, D], F32))
        junk = _es.enter_context(nc.sbuf_tensor("s_junk", [128, D], F32))
        st0 = _es.enter_context(nc.sbuf_tensor("s_st0", [128, 48], F32))
        cload(nwb[:, :], dr["nw_b"], "nwb", False)
        def s0_a(tt):
            xt, xn = xts[tt % 4], xns[tt % 4]
            bx, bn = B("xt%d" % (tt % 4)), B("xn%d" % (tt % 4))
            S.dma("sp", xt[:, :], xv[tt], writes=[bx])
            S.op("act", lambda e: e.activation(junk[:, :], xt[:, :], AF.Square, accum_out=st0[:, tt:tt + 1]),
                 reads=[bx], writes=[B("junk"), B("ss%d" % tt)])
            S.op("act", lambda e: e.activation(st0[:, 16 + tt:17 + tt], st0[:, tt:tt + 1], AF.Sqrt, bias=EPS, scale=1.0 / D),
                 reads=[B("ss%d" % tt)], writes=[B("sd%d" % tt)])
            S.op("dve", lambda e: e.reciprocal(st0[:, 32 + tt:33 + tt], st0[:, 16 + tt:17 + tt]),
                 reads=[B("sd%d" % tt)], writes=[B("rs%d" % tt)])
            S.op("dve", lambda e: e.scalar_tensor_tensor(out=xn[:, :], in0=xt[:, :], scalar=st0[:, 32 + tt:33 + tt],
                                                         in1=nwb[:, :], op0=ALU.mult, op1=ALU.mult),
                 reads=[bx, B("rs%d" % tt), B("nwb")], writes=[bn])

        def s0_b(tt):
            xn, bn = xns[tt % 4], B("xn%d" % (tt % 4))
            pb = tt % 4
            pv = psbf(pb)
            S.mm([lambda e, k=k: e.transpose(pv[:, k * 128:(k + 1) * 128], xn[:, k * 128:(k + 1) * 128], ident[:, :]) for k in range(8)],
                 reads=[bn, B("ident")], writes=[B(PB[pb])])
            S.op("act", lambda e: e.copy(hT[:, :, tt * 128:(tt + 1) * 128], pv.rearrange("p (k t) -> p k t", k=8)),
                 reads=[B(PB[pb])], writes=[B("hT%d" % (tt // 4))])

        s0_a(0)
        s0_a(1)
        for tt in range(16):
            if tt + 2 < 16:
                s0_a(tt + 2)
            s0_b(tt)
        S.barrier()
    if stages <= 0:
        _dbg_out(nc, S, dbg, debug, hT=hT)
        S.finish()
        return nc

    return _build_rest(nc, S, dr, out_d, dbg, debug, stages, PS, PB, psbf, hT, obT, wblock, proj_fm, ident, tri_up, tri_lo,
                       gates, KC, VC, stat, cload, cwT, cbT)


def _conv(nc, S, PS, PB, hT, out_aT, wblock, proj_fm, cwT, cbT):
    B = S.buf
    with ExitStack() as _es:
        u = _es.enter_context(nc.sbuf_tensor("s_u", [128, 2 + T], F32))
        has0 = _es.enter_context(nc.sbuf_tensor("s_has0", [128, 512], F32))
        has1 = _es.enter_context(nc.sbuf_tensor("s_has1", [128, 512], F32))
        sz0 = _es.enter_context(nc.sbuf_tensor("s_sz0", [128, 512], F32))
        sz1 = _es.enter_context(nc.sbuf_tensor("s_sz1", [128, 512], F32))
        y0 = _es.enter_context(nc.sbuf_tensor("s_y0", [128, 512], F32))
        y1 = _es.enter_context(nc.sbuf_tensor("s_y1", [128, 512], F32))
        S.op("dve", lambda e: e.memset(u[:, 0:2], 0.0), writes=[B("upad")])
        it = 0
        for cp in range(4):
            blks = [wblock([(off + 256 * cp, 256)]) for off in (O_HA, O_B, O_C, O_ZA)]
            for cl in range(2):
                ct = 2 * cp + cl
                for tg in range(4):
                    par = it % 2
                    it += 1
                    pbase = 4 * par
                    for j in range(4):
                        proj_fm(PS[pbase + j][:, :], PB[pbase + j], blks[j][0], blks[j][1], cl * 128, 128, tg)
                    has, sz, y = [has0, has1][par], [sz0, sz1][par], [y0, y1][par]
                    bh, bs, by = B("has%d" % par), B("sz%d" % par), B("y%d" % par)
                    S.op("act", lambda e, has=has, pbase=pbase: e.copy(has[:, :], PS[pbase][:, :]), reads=[B(PB[pbase])], writes=[bh])
                    S.op("act", lambda e, sz=sz, pbase=pbase: e.activation(sz[:, :], PS[pbase + 3][:, :], AF.Silu), reads=[B(PB[pbase + 3])], writes=[bs])
                    o = 2 + tg * 512
                    ub = [B("u%d" % tg)]
                    ur = [B("u%d" % tg), B("upad")] + ([B("u%d" % (tg - 1))] if tg else [])
                    S.op("dve", lambda e, has=has, pbase=pbase, o=o: e.tensor_tensor(u[:, o:o + 512], PS[pbase + 2][:, :], has[:, :], op=ALU.mult),
                         reads=[B(PB[pbase + 2]), bh], writes=ub)
                    S.op("dve", lambda e, y=y, o=o, ct=ct: e.tensor_scalar(y[:, :], u[:, o:o + 512], cwT[:, ct, 2:3], cbT[:, ct:ct + 1], op0=ALU.mult, op1=ALU.add),
                         reads=ur + [B("cwT"), B("cbT")], writes=[by])
                    S.op("dve", lambda e, y=y, o=o, ct=ct: e.scalar_tensor_tensor(out=y[:, :], in0=u[:, o - 1:o + 511], scalar=cwT[:, ct, 1:2], in1=y[:, :], op0=ALU.mult, op1=ALU.add),
                         reads=ur + [by], writes=[by])
                    S.op("dve", lambda e, y=y, o=o, ct=ct: e.scalar_tensor_tensor(out=y[:, :], in0=u[:, o - 2:o + 510], scalar=cwT[:, ct, 0:1], in1=y[:, :], op0=ALU.mult, op1=ALU.add),
                         reads=ur + [by], writes=[by])
                    S.op("dve", lambda e, y=y, pbase=pbase: e.tensor_tensor(y[:, :], PS[pbase + 1][:, :], y[:, :], op=ALU.mult),
                         reads=[B(PB[pbase + 1]), by], writes=[by])
                    S.op("dve", lambda e, y=y, sz=sz, ct=ct, tg=tg: e.tensor_tensor(out_aT[:, ct, tg * 512:(tg + 1) * 512], y[:, :], sz[:, :], op=ALU.mult),
                         reads=[by, bs], writes=[B("out_aT")])
        S.barrier()


def _dbg_out(nc, S, dbg, debug, **tiles):
    if not debug:
        return
    for n, t in tiles.items():
        shp = list(t.shape)
        d = nc.dram_tensor("dbg_" + n, shp, t.dtype, kind="ExternalOutput").ap()
        S.barrier()
        S.dma("sp", d, t[tuple(slice(None) for _ in shp)], is_out=True)


def _build_rest(nc, S, dr, out_d, dbg, debug, stages, PS, PB, psbf, hT, obT, wblock, proj_fm, ident, tri_up, tri_lo,
                gates, KC, VC, stat, cload, cwT, cbT):
    B = S.buf

    def hT_tt(k, tt):
        return hT[:, k, tt * 128:(tt + 1) * 128]

    gblk, gbb = wblock([(O_GN, 48)])
    for tt in range(16):
        pb = tt % 2
        S.mm([lambda e, k=k, tt=tt, pb=pb: e.matmul(PS[pb][:, 0:48], hT_tt(k, tt), gblk[:, k, 0:48], start=(k == 0), stop=(k == 7)) for k in range(8)],
             reads=[gbb, B("hT%d" % (tt // 4))], writes=[B(PB[pb])])
        S.op("act", lambda e, tt=tt, pb=pb: e.activation(gates[:, tt, :], PS[pb][:, 0:48], AF.Sigmoid), reads=[B(PB[pb])], writes=[B("gates")])

    S.op("dve", lambda e: e.memset(KC[96:128, :], 0.0), writes=[B("KC")])
    cload(KC[64:104, :], dr["kaug_c"], "KC", True)
    cload(VC[:, :, 64:97], dr["vcc"], "VC", True)
    with ExitStack() as _es:
        w1k = _es.enter_context(nc.sbuf_tensor("s_w1k", [64, 32, 128], BF16))
        w1v = _es.enter_context(nc.sbuf_tensor("s_w1v", [64, 32, 128], BF16))
        w2k = _es.enter_context(nc.sbuf_tensor("s_w2k", [128, 64], BF16))
        w2v = _es.enter_context(nc.sbuf_tensor("s_w2v", [128, 64], BF16))
        pek = _es.enter_context(nc.sbuf_tensor("s_pek", [64, 64], BF16))
        pev = _es.enter_context(nc.sbuf_tensor("s_pev", [64, 64], BF16))
        kcT = _es.enter_context(nc.sbuf_tensor("s_kcT", [64, 16, 128], BF16))
        vcT = _es.enter_context(nc.sbuf_tensor("s_vcT", [64, 16, 128], BF16))
        kcT1 = _es.enter_context(nc.sbuf_tensor("s_kcT1", [64, 16, 128], BF16))
        kcT2 = _es.enter_context(nc.sbuf_tensor("s_kcT2", [64, 16, 128], BF16))
        kcT3 = _es.enter_context(nc.sbuf_tensor("s_kcT3", [64, 16, 128], BF16))
        vcT1 = _es.enter_context(nc.sbuf_tensor("s_vcT1", [64, 16, 128], BF16))
        vcT2 = _es.enter_context(nc.sbuf_tensor("s_vcT2", [64, 16, 128], BF16))
        vcT3 = _es.enter_context(nc.sbuf_tensor("s_vcT3", [64, 16, 128], BF16))
        hk1 = _es.enter_context(nc.sbuf_tensor("s_hk1", [128, 128], BF16))
        hv1 = _es.enter_context(nc.sbuf_tensor("s_hv1", [128, 128], BF16))
        hk = _es.enter_context(nc.sbuf_tensor("s_hk", [128, 128], BF16))
        hv = _es.enter_context(nc.sbuf_tensor("s_hv", [128, 128], BF16))
        cload(w1k[:, :, :], dr["w1k"].rearrange("(p d) h -> d p h", d=64), "w1k", True)
        cload(w1v[:, :, :], dr["w1v"].rearrange("(p d) h -> d p h", d=64), "w1v", True)
        cload(w2k[:, :], dr["w2k"], "w2k", True)
        cload(w2v[:, :], dr["w2v"], "w2v", True)
        cload(pek[:, :], dr["peT_k"], "pek", True)
        cload(pev[:, :], dr["peT_v"], "pev", True)
        kcTs, vcTs = [kcT, kcT1, kcT2, kcT3], [vcT, vcT1, vcT2, vcT3]
        hks, hvs = [hk, hk1], [hv, hv1]
        for g in range(NG):
            blk, bb = wblock([(O_KC + 64 * g, 64), (O_VC + 64 * g, 64)])
            for (dst, c0_, nm) in ((kcTs[g], 0, "kcT%d" % g), (vcTs[g], 64, "vcT%d" % g)):
                for tg in range(4):
                    pb = tg
                    proj_fm(PS[pb][0:64, :], PB[pb], blk, bb, c0_, 64, tg)
                    S.op("act" if tg % 2 else "dve",
                         (lambda e, dst=dst, tg=tg, pb=pb: e.copy(dst[:, :, tg * 32:(tg + 1) * 32], PS[pb][0:64, :].rearrange("d (c p) -> d p c", p=16))) if tg % 2 else
                         (lambda e, dst=dst, tg=tg, pb=pb: e.tensor_copy(dst[:, :, tg * 32:(tg + 1) * 32], PS[pb][0:64, :].rearrange("d (c p) -> d p c", p=16))),
                         reads=[B(PB[pb])], writes=[B(nm)])
        for (w1, pe, col, nm) in ((w1k, pek, 48, "k"), (w1v, pev, 49, "v")):
            S.mm([lambda e, p=p, w1=w1, pe=pe: e.matmul(PS[0][:, 0:2], w1[:, p, :], pe[:, 2 * p:2 * p + 2], start=(p == 0), stop=(p == 31)) for p in range(32)],
                 reads=[B("w1" + nm), B("pe" + nm)], writes=[B(PB[0])])
            S.op("act", lambda e, col=col: e.copy(stat[:, col:col + 1], PS[0][:, 0:1]), reads=[B(PB[0])], writes=[B("cb" + nm)])
        for g in range(NG):
            par = g % 2
            for (src, w1, hid, col, nm, pbk) in ((kcTs[g], w1k, hks[par], 48, "k", 4 + 2 * par), (vcTs[g], w1v, hvs[par], 49, "v", 5 + 2 * par)):
                S.mm([lambda e, p=p, w1=w1, src=src, pbk=pbk: e.matmul(PS[pbk][:, 0:NCMP], w1[:, p, :], src[:, p % 16, (p // 16):(p // 16) + NCMP], start=(p == 0), stop=(p == 31)) for p in range(32)],
                     reads=[B("w1" + nm), B("%scT%d" % (nm, g))], writes=[B(PB[pbk])])
                S.op("act", lambda e, hid=hid, col=col, pbk=pbk: e.activation(hid[:, 0:NCMP], PS[pbk][:, 0:NCMP], AF.Silu, bias=stat[:, col:col + 1]),
                     reads=[B(PB[pbk]), B("cb" + nm)], writes=[B("hid%s%d" % (nm, par))])
            S.mm([lambda e: e.matmul(PS[0 + 2 * par][0:64, 0:NCMP], w2k[:, :], hks[par][:, 0:NCMP], start=True, stop=True)],
                 reads=[B("w2k"), B("hidk%d" % par)], writes=[B(PB[0 + 2 * par])])
            S.op("dve", lambda e: e.tensor_copy(KC[0:64, g * NCMP:(g + 1) * NCMP], PS[0 + 2 * par][0:64, 0:NCMP]), reads=[B(PB[0 + 2 * par])], writes=[B("KC")])
            S.mm([lambda e: e.matmul(PS[1 + 2 * par][0:NCMP, 0:64], hvs[par][:, 0:NCMP], w2v[:, :], start=True, stop=True)],
                 reads=[B("w2v"), B("hidv%d" % par)], writes=[B(PB[1 + 2 * par])])
            S.op("dve", lambda e: e.tensor_copy(VC[:, g, 0:64], PS[1 + 2 * par][0:NCMP, 0:64]), reads=[B(PB[1 + 2 * par])], writes=[B("VC")])
        wblock.prefetch(4)
        S.barrier()
    if stages <= 1:
        _dbg_out(nc, S, dbg, debug, hT=hT, gates=gates, KC=KC, VC=VC)
        S.finish()
        return nc
    _attention(nc, S, dr, PS, PB, psbf, hT, obT, wblock, proj_fm, ident, tri_up, tri_lo, gates, KC, VC, cload, hT_tt)
    if stages <= 2:
        _dbg_out(nc, S, dbg, debug, obT=obT)
        S.finish()
        return nc
    with nc.sbuf_tensor("s_out_aT", [128, 8, T], BF16) as out_aT:
        _conv(nc, S, PS, PB, hT, out_aT, wblock, proj_fm, cwT, cbT)
        if stages <= 3:
            _dbg_out(nc, S, dbg, debug, obT=obT, out_aT=out_aT)
            S.finish()
            return nc
        _tail(nc, S, dr, out_d, PS, PB, hT, out_aT, obT, wblock, proj_fm, cload)
    S.finish()
    return nc


def _attention(nc, S, dr, PS, PB, psbf, hT, obT, wblock, proj_fm, ident, tri_up, tri_lo, gates, KC, VC, cload, hT_tt):
    B = S.buf
    T_ = lambda n, shp, dt: nc.sbuf_tensor('s_' + n, shp, dt)
    with ExitStack() as _es:
        Qs0 = _es.enter_context(T_("Qs0", [128, T], BF16))
        Qs1 = _es.enter_context(T_("Qs1", [128, T], BF16))
        Qs2 = _es.enter_context(T_("Qs2", [128, T], BF16))
        Qs3 = _es.enter_context(T_("Qs3", [128, T], BF16))
        KS = _es.enter_context(T_("KS", [128, T], BF16))
        KW = _es.enter_context(T_("KW", [128, T], BF16))
        V2 = _es.enter_context(T_("V2", [128, 16, 2, 65], BF16))
        zsT = _es.enter_context(T_("zsT", [128, 2, T], F32))
        O_st0 = _es.enter_context(T_("O_st0", [128, 4, 4, 3, 65], F32))
        O_st1 = _es.enter_context(T_("O_st1", [128, 4, 4, 3, 65], F32))
        Pb0 = _es.enter_context(T_("Pb0", [128, 512], BF16))
        Pb1 = _es.enter_context(T_("Pb1", [128, 512], BF16))
        Pb2 = _es.enter_context(T_("Pb2", [128, 512], BF16))
        cmask = _es.enter_context(T_("cmask", [128, T], BF16))
        fimp = _es.enter_context(T_("fimp", [128, 16, 32], BF16))
        impacc = _es.enter_context(T_("impacc", [128, 4, 32], F32))
        impm = _es.enter_context(T_("impm", [128, 4, 32], F32))
        top8 = _es.enter_context(T_("top8", [128, 4, 8], F32))
        negw = _es.enter_context(T_("negw", [128, 4, 128], BF16))
        rsc = _es.enter_context(T_("rsc", [128, 8], F32))
        coefs = [_es.enter_context(T_("coef_%d" % i, [128, 4, 3], F32)) for i in range(4)]
        coef2s = [_es.enter_context(T_("coefb_%d" % i, [128, 4, 3], F32)) for i in range(4)]
        prod = _es.enter_context(T_("prod", [128, 4, 3, 64], F32))
        o4 = _es.enter_context(T_("o4", [128, 4, 64], F32))
        Pb3 = _es.enter_context(T_("Pb3", [128, 512], BF16))
        Pb4 = _es.enter_context(T_("Pb4", [128, 512], BF16))
        negw2 = _es.enter_context(T_("negw2", [128, 4, 128], BF16))
        cst0 = _es.enter_context(T_("cst0", [128, 4, 97], F32))
        cst1 = _es.enter_context(T_("cst1", [128, 4, 97], F32))
        obt0 = _es.enter_context(T_("obt0", [128, 256], BF16))
        obt1 = _es.enter_context(T_("obt1", [128, 256], BF16))
        obt2 = _es.enter_context(T_("obt2", [128, 256], BF16))
        obt3 = _es.enter_context(T_("obt3", [128, 256], BF16))
        Qs = [Qs0, Qs1, Qs2, Qs3]
        O_sts = [O_st0, O_st1]
        Pbs = [Pb0, Pb1, Pb2, Pb3, Pb4]
        negws = [negw, negw2]
        st = {"s": 0, "p": 0, "m": 0, "a": 0}
        pend = []
        deferred = []
        defq = []

        def need(level):
            keep = []
            for (t, f) in defq:
                if t <= level:
                    f()
                else:
                    keep.append((t, f))
            defq[:] = keep

        def filler():
            if defq:
                t, f = defq.pop(0)
                f()

        obts = [obt0, obt1, obt2, obt3]
        cst = [cst0, cst1]

        def flush(keep=0):
            while len(pend) > keep:
                ppv, ppost = pend.pop(0)
                pf, pr, pw = ppv()
                S.mm(pf, reads=pr, writes=pw)
                if ppost:
                    ppost()

        def nxt_s():
            st["s"] = (st["s"] + 1) % 3
            return st["s"]

        def nxt_p():
            st["p"] = (st["p"] + 1) % 5
            return st["p"]

        def nxt_m():
            st["m"] ^= 1
            return 6 + st["m"]

        S.op("dve", lambda e: e.memset(KS[96:128, :], 0.0), writes=[B("KSc")])
        S.op("dve", lambda e: e.memset(KW[96:128, :], 0.0), writes=[B("KWc")])
        for hl in range(4):
            S.op("dve", lambda e, hl=hl: e.memset(Qs[hl][96:128, :], 0.0), writes=[B("Qal%d" % hl)])
        cload(KS[64:104, :], dr["kaug_s"], "KSc", True)
        cload(KW[64:104, :], dr["kaug_w"], "KWc", True)
        cload(cmask[:, :], dr["cmask"], "cmask", True)
        cload(fimp[:, :, :], dr["fimp"], "fimp", True)
        for hl in range(4):
            for tg in range(4):
                S.op("dve", lambda e, hl=hl, tg=tg: e.memset(Qs[hl][64:96, tg * 512:(tg + 1) * 512], 0.0), writes=[B("N%d_%d" % (hl, tg))])
        S.op("dve", lambda e: e.memset(V2[:, :, :, 64:65], 1.0), writes=[B("V2one")])
        S.op("dve", lambda e: e.memset(negw[:, :, :], 0.0), writes=[B("negw0")])
        S.op("dve", lambda e: e.memset(negw2[:, :, :], 0.0), writes=[B("negw1")])

        def qreads(hl, tg):
            return [B("Q%d_%d" % (hl, tg)), B("N%d_%d" % (hl, tg)), B("Qal%d" % hl)]

        for g in range(NG):
            bq, bqb = wblock([(O_Q + 256 * g, 256)])
            bk, bkb = wblock([(O_KS + 64 * g, 64), (O_KW + 64 * g, 64)])
            bz, bzb = wblock([(O_ZB + 256 * g, 256)])
            bv, bvb = wblock([(O_VS + 64 * g, 64), (O_VW + 64 * g, 64)])
            for hl in range(4):
                cload(Qs[hl][96:104, :], dr["qal"][4 * g + hl], "Qal%d" % hl, False)
            FB = 5

            def q_item(j, tg, pb):
                proj_fm(PS[pb][:, :], PB[pb], bq, bqb, j * 128, 128, tg)
                S.op("dve", lambda e: e.tensor_scalar(Qs[2 * j][0:64, tg * 512:(tg + 1) * 512], PS[pb][0:64, :], 0.125, None, op0=ALU.mult),
                     reads=[B(PB[pb])], writes=[B("Q%d_%d" % (2 * j, tg))])
                S.op("dve", lambda e: e.tensor_scalar(Qs[2 * j + 1][0:64, tg * 512:(tg + 1) * 512], PS[pb][64:128, :], 0.125, None, op0=ALU.mult),
                     reads=[B(PB[pb])], writes=[B("Q%d_%d" % (2 * j + 1, tg))])

            def k_item(tg, pb):
                proj_fm(PS[pb][:, :], PB[pb], bk, bkb, 0, 128, tg)
                S.op("dve", lambda e: e.tensor_copy(KS[0:64, tg * 512:(tg + 1) * 512], PS[pb][0:64, :]), reads=[B(PB[pb])], writes=[B("KS_%d" % tg)])
                S.op("dve", lambda e: e.tensor_copy(KW[0:64, tg * 512:(tg + 1) * 512], PS[pb][64:128, :]), reads=[B(PB[pb])], writes=[B("KW_%d" % tg)])

            def v_item(tt, pb):
                S.mm([lambda e, k=k: e.matmul(PS[pb][:, 0:128], hT_tt(k, tt), bv[:, k, 0:128], start=(k == 0), stop=(k == 7)) for k in range(8)],
                     reads=[bvb, B("hT%d" % (tt // 4))], writes=[B(PB[pb])])
                S.op("dve", lambda e: e.tensor_copy(V2[:, tt, :, 0:64], PS[pb][:, 0:128].rearrange("p (b d) -> p b d", b=2)),
                     reads=[B(PB[pb])], writes=[B("V2_%d" % (tt // 4))])

            def z_item(j, tg, pb):
                proj_fm(PS[pb][:, :], PB[pb], bz, bzb, j * 128, 128, tg)
                S.op("act", lambda e: e.activation(zsT[:, j, tg * 512:(tg + 1) * 512], PS[pb][:, :], AF.Silu), reads=[B(PB[pb])], writes=[B("zsil_%d" % tg)])

            for f in deferred:
                f()
            del deferred[:]
            for j in range(2):
                q_item(j, 0, nxt_m())
            k_item(0, nxt_m())
            for tt in range(4):
                v_item(tt, nxt_m())
            del defq[:]
            for tg in range(1, 4):
                for j in range(2):
                    defq.append((tg, lambda j=j, tg=tg: q_item(j, tg, FB)))
                defq.append((tg, lambda tg=tg: k_item(tg, FB)))
                for tt in range(4 * tg, 4 * tg + 4):
                    defq.append((tg, lambda tt=tt: v_item(tt, FB)))
                for j in range(2):
                    defq.append((tg, lambda j=j, tg=tg: z_item(j, tg - 1, FB)))
            for j in range(2):
                defq.append((4, lambda j=j: z_item(j, 3, FB)))
            wblock.prefetch(4)
            def run_step(smm, exp, pv, pre=None, post=None):
                if pre:
                    pre()
                sf, sr, sw = smm()
                if len(pend) >= 3:
                    ppv, ppost = pend.pop(0)
                    pf, pr, pw = ppv()
                    S.mm(sf + pf, reads=sr + pr, writes=sw + pw)
                    exp()
                    if ppost:
                        ppost()
                else:
                    S.mm(sf, reads=sr, writes=sw)
                    exp()
                pend.append((pv, post))
                filler()

            def unit(kind, hl, tg, pre_first=None, post_last=None):
                KA, kan, kcn, vb, bidx = (KS, "KS", "KSc", 0, 1) if kind == "s" else (KW, "KW", "KWc", 1, 2)
                kts = list(range(0, 4 * tg + 4)) if kind == "s" else list(range(max(0, 4 * tg - 4), 4 * tg + 4))
                st["a"] = (st["a"] + 1) % 2
                acc = 3 + st["a"]
                for idx, kt in enumerate(kts):
                    r = kt - 4 * tg
                    i_lo = max(0, r)
                    i_hi = 3 if kind == "s" else min(3, r + 4)
                    c0, c1 = i_lo * 128, (i_hi + 1) * 128
                    sbk = nxt_s()
                    pi = nxt_p()
                    Pb = Pbs[pi]

                    def smm(kt=kt, r=r, c0=c0, c1=c1, sbk=sbk):
                        specs = [(PS[sbk][:, c0:c1], KA[:, kt * 128:(kt + 1) * 128], Qs[hl][:, tg * 512 + c0:tg * 512 + c1])]
                        rd = [B("%s_%d" % (kan, kt // 4)), B(kcn)] + qreads(hl, tg)
                        if r >= 0:
                            specs.append((PS[sbk][:, r * 128:(r + 1) * 128], ident[:, :], tri_up[:, :]))
                            rd += [B("ident"), B("tri_up")]
                        if kind == "w" and 0 <= r + 4 <= 3:
                            specs.append((PS[sbk][:, (r + 4) * 128:(r + 5) * 128], ident[:, :], tri_lo[:, :]))
                            rd += [B("ident"), B("tri_lo")]
                        n = len(specs)
                        return ([lambda e, sp=sp, j=j: e.matmul(sp[0], sp[1], sp[2], start=(j == 0), stop=(j == n - 1)) for j, sp in enumerate(specs)],
                                rd, [B(PB[sbk])])

                    def exp(c0=c0, c1=c1, sbk=sbk, pi=pi, Pb=Pb):
                        S.op("act", lambda e: e.activation(Pb[:, c0:c1], PS[sbk][:, c0:c1], AF.Exp), reads=[B(PB[sbk])], writes=[B("Pb%d" % pi)])

                    def pv(kt=kt, i_lo=i_lo, i_hi=i_hi, idx=idx, pi=pi, Pb=Pb):
                        fns = []
                        for i in range(i_lo, i_hi + 1):
                            fns.append(lambda e, i=i: e.matmul(PS[acc][:, i * 65:(i + 1) * 65], Pb[:, i * 128:(i + 1) * 128], V2[:, kt, vb, :],
                                                               start=(idx == 0 and i == i_lo), stop=(kt == 4 * tg + i), skip_group_check=True))
                        return (fns, [B("Pb%d" % pi), B("V2_%d" % (kt // 4)), B("V2one")], [B(PB[acc])])

                    post = None
                    if idx == len(kts) - 1:
                        def post():
                            S.op("dve", lambda e: e.tensor_copy(O_sts[tg % 2][:, :, hl, bidx, :], PS[acc][:, 0:260].rearrange("p (i c) -> p i c", i=4)),
                                 reads=[B(PB[acc])], writes=[B("O_st%d" % (tg % 2))])
                            if post_last:
                                post_last()
                    run_step(smm, exp, pv, pre=pre_first if idx == 0 else None, post=post)

            def cmp_step(hl, tg, post_last=None):
                cols = slice(tg * 512, (tg + 1) * 512)
                sbk = nxt_s()
                pi = nxt_p()
                Pb = Pbs[pi]
                mb = nxt_m()
                mv = PS[mb][:, 0:388].rearrange("p (i c) -> p i c", i=4)

                def smm():
                    return ([lambda e: e.matmul(PS[sbk][0:NCMP, :], KC[:, g * NCMP:(g + 1) * NCMP], Qs[hl][:, cols], start=True, stop=False),
                             lambda e: e.matmul(PS[sbk][0:NCMP, :], ident[0:NCMP, 0:NCMP], cmask[0:NCMP, cols], start=False, stop=True)],
                            [B("KC"), B("ident"), B("cmask")] + qreads(hl, tg), [B(PB[sbk])])

                def exp():
                    S.op("act", lambda e: e.activation(Pb[0:NCMP, :], PS[sbk][0:NCMP, :], AF.Exp), reads=[B(PB[sbk])], writes=[B("Pb%d" % pi)])

                def pv():
                    return ([lambda e, i=i: e.matmul(PS[mb][:, i * 97:(i + 1) * 97], Pb[0:NCMP, i * 128:(i + 1) * 128], VC[0:NCMP, g, :], start=True, stop=True) for i in range(4)],
                            [B("Pb%d" % pi), B("VC")], [B(PB[mb])])

                def post():
                    cs = cst[hl % 2]
                    csn = "cst%d" % (hl % 2)
                    S.op("act", lambda e: e.copy(cs[:, :, :], mv[:, :, :]), reads=[B(PB[mb])], writes=[B(csn)])
                    S.op("dve", lambda e: e.tensor_copy(O_sts[tg % 2][:, :, hl, 0, :], cs[:, :, 0:65]), reads=[B(csn)], writes=[B("O_c%d" % (tg % 2))])
                    S.op("dve", lambda e: e.tensor_scalar(rsc[:, 0:4], cs[:, :, 64], 1e-30, None, op0=ALU.max), reads=[B(csn)], writes=[B("rsc")])
                    S.op("dve", lambda e: e.reciprocal(rsc[:, 4:8], rsc[:, 0:4]), reads=[B("rsc")], writes=[B("rsc2")])
                    if hl == 0:
                        S.op("dve", lambda e: e.tensor_tensor(impacc[:, :, :], cs[:, :, 65:97], rsc[:, 4:8].unsqueeze(2).to_broadcast([128, 4, 32]), op=ALU.mult),
                             reads=[B(csn), B("rsc2")], writes=[B("impacc")])
                    else:
                        S.op("dve", lambda e: e.tensor_tensor(cs[:, :, 65:97], cs[:, :, 65:97], rsc[:, 4:8].unsqueeze(2).to_broadcast([128, 4, 32]), op=ALU.mult),
                             reads=[B(csn), B("rsc2")], writes=[B(csn)])
                        S.op("dve", lambda e: e.tensor_tensor(impacc[:, :, :], impacc[:, :, :], cs[:, :, 65:97], op=ALU.add),
                             reads=[B(csn), B("impacc")], writes=[B("impacc")])
                    if post_last:
                        post_last()
                run_step(smm, exp, pv, post=post)

            for tg in range(4):
                cols = slice(tg * 512, (tg + 1) * 512)

                def selection(tg=tg):
                    S.op("dve", lambda e: e.tensor_tensor(impm[:, :, :], impacc[:, :, :], fimp[:, 4 * tg:4 * tg + 4, :], op=ALU.add),
                         reads=[B("impacc"), B("fimp")], writes=[B("impm")])
                    for i in range(4):
                        S.op("dve", lambda e, i=i: e.max(top8[:, i, :], impm[:, i, :]), reads=[B("impm")], writes=[B("top8")])
                    nw = negws[tg % 2]
                    for i in range(4):
                        S.op("dve", lambda e, i=i: e.tensor_scalar(nw[:, i, 64:96], impm[:, i, :], top8[:, i, 7:8], NEG, op0=ALU.is_lt, op1=ALU.mult),
                             reads=[B("impm"), B("top8")], writes=[B("negw%d" % (tg % 2))])

                sel_next = (lambda tg=tg: selection(tg + 1))

                def sel_to_q(tg=tg, cols=cols):
                    for f in deferred:
                        f()
                    del deferred[:]
                    mb = nxt_m()
                    pv_ = psbf(mb)
                    nw = negws[tg % 2]
                    S.mm([lambda e, i=i: e.transpose(pv_[:, i * 128:(i + 1) * 128], nw[:, i, :], ident[:, :]) for i in range(4)],
                         reads=[B("negw%d" % (tg % 2)), B("ident")], writes=[B(PB[mb])])
                    for hl in range(4):
                        S.op("dve", lambda e, hl=hl: e.tensor_copy(Qs[hl][64:96, cols], pv_[64:96, 0:512]), reads=[B(PB[mb])], writes=[B("N%d_%d" % (hl, tg))])

                def flush_deferred():
                    for f in deferred:
                        f()
                    del deferred[:]

                def normalize(tg=tg):
                    O_st = O_sts[tg % 2]
                    osn = "O_st%d" % (tg % 2)
                    ocn = "O_c%d" % (tg % 2)
                    for i in range(4):
                        tt = 4 * tg + i
                        cf, cf2 = coefs[i], coef2s[i]
                        S.op("dve", lambda e, i=i, cf=cf: e.tensor_scalar(cf[:, :, :], O_st[:, i, :, :, 64], 1e-30, None, op0=ALU.max), reads=[B(osn), B(ocn)], writes=[B("coef%d" % i)])
                        S.op("dve", lambda e, cf=cf, cf2=cf2: e.reciprocal(cf2[:, :, :], cf[:, :, :]), reads=[B("coef%d" % i)], writes=[B("coefb%d" % i)])
                        S.op("dve", lambda e, tt=tt, cf2=cf2: e.tensor_tensor(cf2[:, :, :], cf2[:, :, :], gates[:, tt, 12 * g:12 * g + 12].rearrange("p (h b) -> p h b", b=3), op=ALU.mult),
                             reads=[B("coefb%d" % i), B("gates")], writes=[B("coefb%d" % i)])
                    for i in range(4):
                        tt = 4 * tg + i
                        ob_i = obts[i]
                        cf2 = coef2s[i]
                        S.op("pool", lambda e, i=i, cf2=cf2: e.tensor_tensor(prod[:, :, :, :], O_st[:, i, :, :, 0:64], cf2[:, :, :].unsqueeze(3).to_broadcast([128, 4, 3, 64]), op=ALU.mult),
                             reads=[B(osn), B(ocn), B("coefb%d" % i)], writes=[B("prod")])
                        S.op("pool", lambda e: e.tensor_tensor(o4[:, :, :], prod[:, :, 0, :], prod[:, :, 1, :], op=ALU.add), reads=[B("prod")], writes=[B("o4")])
                        S.op("pool", lambda e, ob_i=ob_i: e.tensor_tensor(ob_i[:, :].rearrange("p (h d) -> p h d", h=4), o4[:, :, :], prod[:, :, 2, :], op=ALU.add),
                             reads=[B("prod"), B("o4")], writes=[B("obt%d" % i)])

                        def tr(tt=tt, i=i, ob_i=ob_i, g=g):
                            mb = nxt_m()
                            pv_ = psbf(mb)
                            S.mm([lambda e, j=j: e.transpose(pv_[:, j * 128:(j + 1) * 128], ob_i[:, j * 128:(j + 1) * 128], ident[:, :]) for j in range(2)],
                                 reads=[B("obt%d" % i), B("ident")], writes=[B(PB[mb])])
                            S.op("dve", lambda e: e.tensor_tensor(obT[:, 2 * g:2 * g + 2, tt * 128:(tt + 1) * 128], pv_[:, 0:256].rearrange("p (j t) -> p j t", j=2),
                                                                  zsT[:, :, tt * 128:(tt + 1) * 128], op=ALU.mult),
                                 reads=[B(PB[mb]), B("zsil_%d" % (tt // 4))], writes=[B("obT")])
                        deferred.append(tr)

                def win_pre(tg=tg):
                    pass

                need(tg)
                if tg == 0:
                    for hl in range(4):
                        cmp_step(hl, 0, post_last=selection if hl == 3 else None)
                for hl in range(4):
                    unit("w", hl, tg, pre_first=win_pre if hl == 0 else None)
                    if tg < 3:
                        need(tg + 1)
                        cmp_step(hl, tg + 1, post_last=sel_next if hl == 3 else None)
                for hl in range(4):
                    unit("s", hl, tg, pre_first=sel_to_q if hl == 0 else None, post_last=normalize if hl == 3 else None)
            need(99)
            flush()
        for f in deferred:
            f()
        del deferred[:]
        S.barrier()


def _tail(nc, S, dr, out_d, PS, PB, hT, out_aT, obT, wblock, proj_fm, cload):
    B = S.buf
    T_ = lambda n, shp, dt: nc.sbuf_tensor('s_' + n, shp, dt)
    wv = lambda n: dr[n].rearrange("(kt p) c -> p kt c", p=128)
    with ExitStack() as _es:
        mixT = _es.enter_context(T_("mixT", [128, 8, T], BF16))
        sg0 = _es.enter_context(T_("sg0", [128, 512], F32))
        sg1 = _es.enter_context(T_("sg1", [128, 512], F32))
        sg2 = _es.enter_context(T_("sg2", [128, 512], F32))
        sg3 = _es.enter_context(T_("sg3", [128, 512], F32))
        m0 = _es.enter_context(T_("m0", [128, 512], F32))
        m1 = _es.enter_context(T_("m1", [128, 512], F32))
        xa = _es.enter_context(T_("xa", [128, D], F32))
        xb = _es.enter_context(T_("xb", [128, D], F32))
        xc = _es.enter_context(T_("xc", [128, D], F32))
        fnw = _es.enter_context(T_("fnw", [128, D], F32))
        junk2 = _es.enter_context(T_("junk2", [128, D], BF16))
        oa = _es.enter_context(T_("oa", [128, D], F32))
        ob_ = _es.enter_context(T_("ob_", [128, D], F32))
        st1 = _es.enter_context(T_("st1", [128, 48], F32))
        cload(fnw[:, :], dr["fnw_b"], "fnw", False)
        sgs, ms = [[sg0, sg1], [sg2, sg3]], [m0, m1]
        it = 0
        for cp in range(4):
            ba, bab = wblock([(256 * cp, 256)], src="wpa")
            bb_, bbb = wblock([(256 * cp, 256)], src="wpb")
            bg0, bg0b = wblock([(O_GM + 256 * cp, 256)])
            bg1, bg1b = wblock([(O_GM + D + 256 * cp, 256)])
            for cl in range(2):
                c = 2 * cp + cl
                for tg in range(4):
                    par = it % 2
                    it += 1
                    pb = 4 * par
                    tcols = slice(tg * 512, (tg + 1) * 512)
                    S.mm([lambda e, k=k: e.matmul(PS[pb][:, :], ba[:, k, cl * 128:(cl + 1) * 128], out_aT[:, k, tcols], start=(k == 0), stop=(k == 7)) for k in range(8)],
                         reads=[bab, B("out_aT")], writes=[B(PB[pb])])
                    S.mm([lambda e, k=k: e.matmul(PS[pb + 1][:, :], bb_[:, k, cl * 128:(cl + 1) * 128], obT[:, k, tcols], start=(k == 0), stop=(k == 7)) for k in range(8)],
                         reads=[bbb, B("obT")], writes=[B(PB[pb + 1])])
                    proj_fm(PS[pb + 2][:, :], PB[pb + 2], bg0, bg0b, cl * 128, 128, tg)
                    proj_fm(PS[pb + 3][:, :], PB[pb + 3], bg1, bg1b, cl * 128, 128, tg)
                    s0, s1, m = sgs[par][0], sgs[par][1], ms[par]
                    S.op("act", lambda e: e.activation(s0[:, :], PS[pb + 2][:, :], AF.Sigmoid), reads=[B(PB[pb + 2])], writes=[B("sga%d" % par)])
                    S.op("act", lambda e: e.activation(s1[:, :], PS[pb + 3][:, :], AF.Sigmoid), reads=[B(PB[pb + 3])], writes=[B("sgb%d" % par)])
                    S.op("dve", lambda e: e.tensor_tensor(m[:, :], PS[pb][:, :], s0[:, :], op=ALU.mult), reads=[B(PB[pb]), B("sga%d" % par)], writes=[B("m%d" % par)])
                    S.op("dve", lambda e: e.tensor_tensor(s1[:, :], PS[pb + 1][:, :], s1[:, :], op=ALU.mult), reads=[B(PB[pb + 1]), B("sgb%d" % par)], writes=[B("sgb%d" % par)])
                    S.op("dve", lambda e: e.tensor_tensor(mixT[:, c, tcols], m[:, :], s1[:, :], op=ALU.add), reads=[B("m%d" % par), B("sgb%d" % par)], writes=[B("mixT")])
        bo = [wblock([(256 * j, 256)], src="wo") for j in range(4)]
        xv = dr["x"].rearrange("(tt p) d -> tt p d", p=128)
        ov = out_d.rearrange("(tt p) d -> tt p d", p=128)
        os_ = [oa, ob_]
        xs = [xa, xb, xc]

        def t_load(tt):
            par = tt % 3
            S.dma("act", xs[par][:, :], xv[tt], writes=[B("xt%d" % par)])

        def t_a(tt):
            par = tt % 3
            xt = xs[par]
            pb = 2 * (tt % 4)
            fns = []
            for j in range(4):
                for k in range(8):
                    fns.append(lambda e, j=j, k=k: e.matmul(PS[pb + j // 2][:, (j % 2) * 256:(j % 2 + 1) * 256], mixT[:, k, tt * 128:(tt + 1) * 128], bo[j][0][:, k, :],
                                                            start=(k == 0), stop=(k == 7)))
            S.mm(fns, reads=[B("mixT")] + [b[1] for b in bo], writes=[B(PB[pb]), B(PB[pb + 1])])
            for hf in range(2):
                S.op("dve", lambda e, hf=hf: e.tensor_tensor(xt[:, hf * 512:(hf + 1) * 512], PS[pb + hf][:, :], xt[:, hf * 512:(hf + 1) * 512], op=ALU.add),
                     reads=[B(PB[pb + hf]), B("xt%d" % par)], writes=[B("xt%d" % par)])
            S.op("act", lambda e: e.activation(junk2[:, :], xt[:, :], AF.Square, accum_out=st1[:, tt:tt + 1]),
                 reads=[B("xt%d" % par)], writes=[B("junk2"), B("fs%d" % tt)])
            S.op("act", lambda e: e.activation(st1[:, 16 + tt:17 + tt], st1[:, tt:tt + 1], AF.Sqrt, bias=EPS, scale=1.0 / D), reads=[B("fs%d" % tt)], writes=[B("fd%d" % tt)])

        def t_b(tt):
            par = tt % 3
            xt, ot = xs[par], os_[tt % 2]
            S.op("dve", lambda e: e.reciprocal(st1[:, 32 + tt:33 + tt], st1[:, 16 + tt:17 + tt]), reads=[B("fd%d" % tt)], writes=[B("fr%d" % tt)])
            S.op("dve", lambda e: e.scalar_tensor_tensor(out=ot[:, :], in0=xt[:, :], scalar=st1[:, 32 + tt:33 + tt], in1=fnw[:, :], op0=ALU.mult, op1=ALU.mult),
                 reads=[B("xt%d" % par), B("fr%d" % tt), B("fnw")], writes=[B("ot%d" % (tt % 2))])
            S.dma("sp", ov[tt], ot[:, :], reads=[B("ot%d" % (tt % 2))], is_out=True)

        t_load(0)
        t_load(1)
        t_a(0)
        for tt in range(16):
            if tt + 2 < 16:
                t_load(tt + 2)
            if tt + 1 < 16:
                t_a(tt + 1)
            t_b(tt)
        S.barrier()


_NC_CACHE = {}


def _host_inputs(inputs):
    f = lambda a: np.ascontiguousarray(np.asarray(a, dtype=np.float32))
    l = 0
    shared = {
        "w_in": f(inputs["w_in"][l]),
        "nw_b": f(np.broadcast_to(np.asarray(inputs["norm_w"][l])[None, :], (128, D))),
        "conv_wT": f(np.asarray(inputs["conv_w"][l]).reshape(3, 8, 128).transpose(2, 1, 0)),
        "conv_bT": f(np.asarray(inputs["conv_b"][l]).reshape(8, 128).T),
        "peT_k": f(np.repeat(np.asarray(inputs["cmp_pe_k"][l]).T, 2, axis=1)),
        "peT_v": f(np.repeat(np.asarray(inputs["cmp_pe_v"][l]).T, 2, axis=1)),
        "w1k": f(inputs["cmp_w1_k"][l]), "w1v": f(inputs["cmp_w1_v"][l]),
        "w2k": f(inputs["cmp_w2_k"][l]), "w2v": f(inputs["cmp_w2_v"][l]),
        "wpa": f(inputs["w_proj_a"][l]), "wpb": f(inputs["w_proj_b"][l]), "wo": f(inputs["w_out"][l]),
        "fnw_b": f(np.broadcast_to(np.asarray(inputs["final_norm_w"])[None, :], (128, D))),
    }
    shared.update(_consts())
    return shared


def kernel(**inputs):
    key = "main"
    if key not in _NC_CACHE:
        _NC_CACHE[key] = build_nc()
    nc = _NC_CACHE[key]
    shared = _host_inputs(inputs)
    x = np.asarray(inputs["x"], dtype=np.float32)
    in_maps = []
    for b in range(8):
        m = dict(shared)
        m["x"] = np.ascontiguousarray(x[b])
        in_maps.append(m)
    res = run_bass_kernel_spmd(nc, in_maps, core_ids=list(range(8)))
    return np.stack([np.asarray(r["out"], dtype=np.float32) for r in res.results], axis=0)
```

```python
from contextlib import ExitStack

import ml_dtypes
import numpy as np
import concourse.bass as bass
import concourse.mybir as mybir
from concourse.bass_utils import run_bass_kernel_spmd

F32 = mybir.dt.float32
BF16 = mybir.dt.bfloat16
AF = mybir.ActivationFunctionType
ALU = mybir.AluOpType


class _Buf:
    __slots__ = ("name", "w", "r")

    def __init__(self, name):
        self.name = name
        self.w = None
        self.r = {}


class _Eng:
    def __init__(self, name, handle, sem):
        self.name, self.h, self.sem = name, handle, sem
        self.cnt = 0
        self.known = {}


class Sched:
    def __init__(self, nc, n_dma_sems=48):
        self.nc = nc
        hs = {"pe": nc.tensor, "act": nc.scalar, "dve": nc.vector, "pool": nc.gpsimd, "sp": nc.sync}
        self.eng = {k: _Eng(k, h, nc.alloc_semaphore("sem_" + k)) for k, h in hs.items()}
        self.dsem = [[nc.alloc_semaphore("dsem%d" % i), 0] for i in range(n_dma_sems)]
        n3 = n_dma_sems // 3
        self.dpool = {"sp": self.dsem[:n3], "act": self.dsem[n3:2 * n3], "pool": self.dsem[2 * n3:]}
        self.dnext = {"sp": 0, "act": 0, "pool": 0}
        self.bufs = {}
        self.out_toks = []

    def buf(self, name):
        b = self.bufs.get(name)
        if b is None:
            b = self.bufs[name] = _Buf(name)
        return b

    def _deps(self, reads, writes):
        deps = {}

        def add(t):
            k = id(t[0])
            if k not in deps or deps[k][1] < t[1]:
                deps[k] = t
        for b in reads:
            if b.w is not None:
                add(b.w)
        for b in writes:
            if b.w is not None:
                add(b.w)
            for t in b.r.values():
                add(t)
        return deps

    def _wait(self, e, deps, skip_own=False):
        for k, (sem, val) in deps.items():
            if skip_own and sem is e.sem:
                continue
            if e.known.get(k, 0) >= val:
                continue
            e.h.wait_ge(sem, val)
            e.known[k] = val

    def _mark(self, tok, reads, writes):
        k = id(tok[0])
        for b in reads:
            b.r[k] = tok
        for b in writes:
            b.w = tok
            b.r = {}

    def op(self, en, fn, reads=(), writes=()):
        e = self.eng[en]
        xr = [b for b in reads if b.name.startswith("ps")]
        if xr:
            reads = [b for b in reads if not b.name.startswith("ps")]
            writes = list(writes) + xr
        self._wait(e, self._deps(reads, writes))
        e.cnt += 1
        fn(e.h).then_inc(e.sem, 1)
        tok = (e.sem, e.cnt)
        self._mark(tok, reads, writes)
        return tok

    def mm(self, fns, reads=(), writes=()):
        e = self.eng["pe"]
        self._wait(e, self._deps(reads, writes), skip_own=True)
        for f in fns[:-1]:
            f(e.h)
        e.cnt += 1
        fns[-1](e.h).then_inc(e.sem, 1)
        tok = (e.sem, e.cnt)
        self._mark(tok, reads, writes)
        return tok

    def dma(self, en, out, in_, reads=(), writes=(), is_out=False, **kw):
        e = self.eng[en]
        self._wait(e, self._deps(reads, writes))
        pool = self.dpool[en]
        slot = pool[self.dnext[en]]
        self.dnext[en] = (self.dnext[en] + 1) % len(pool)
        if slot[1] and e.known.get(id(slot[0]), 0) < slot[1]:
            e.h.wait_ge(slot[0], slot[1])
            e.known[id(slot[0])] = slot[1]
        slot[1] += 16
        e.h.dma_start(out=out, in_=in_, **kw).then_inc(slot[0], 16)
        tok = (slot[0], slot[1])
        self._mark(tok, reads, writes)
        if is_out:
            self.out_toks.append(tok)
        return tok

    def barrier(self):
        toks = {}

        def add(t):
            k = id(t[0])
            if k not in toks or toks[k][1] < t[1]:
                toks[k] = t
        for e in self.eng.values():
            if e.cnt:
                add((e.sem, e.cnt))
        keep = {}
        for n, b in self.bufs.items():
            if n.startswith("wblk"):
                keep[n] = b
                continue
            if b.w is not None:
                add(b.w)
            for t in b.r.values():
                add(t)
        for e in self.eng.values():
            for k, (sem, val) in toks.items():
                if e.known.get(k, 0) < val:
                    e.h.wait_ge(sem, val)
                    e.known[k] = val
        self.bufs = keep

    def finish(self):
        e = self.eng["sp"]
        for sem, val in self.out_toks:
            if e.known.get(id(sem), 0) < val:
                e.h.wait_ge(sem, val)
                e.known[id(sem)] = val


D = 1024
T = 2048
NH = 16
NG = 4
HD = 64
N_IN = 9776
NEG = -30000.0
EPS = 1e-6
O_HA, O_B, O_C, O_ZA, O_Q = 0, 1024, 2048, 3072, 4096
O_KC, O_VC, O_KS, O_VS, O_KW, O_VW = 5120, 5376, 5632, 5888, 6144, 6400
O_GN, O_ZB, O_GM = 6656, 6704, 7728
NCMP = 127
DEBUG = False


def _bf16r(x):
    u = np.ascontiguousarray(x, dtype=np.float32).view(np.uint32).astype(np.uint64)
    r = ((u + 0x7FFF + ((u >> 16) & 1)) >> 16) << 16
    return (r & 0xFFFFFFFF).astype(np.uint32).view(np.float32)


def _consts():
    c = {}
    c["ident"] = np.eye(128, dtype=np.float32)
    k = np.arange(128)[:, None]
    t = np.arange(128)[None, :]
    c["tri_up"] = np.where(k <= t, 0.0, NEG).astype(np.float32)
    c["tri_lo"] = np.where(k > t, 0.0, NEG).astype(np.float32)
    cend = 16 * np.arange(128) + 31
    tt = np.arange(T)
    cm = np.where(cend[:, None] <= tt[None, :], 0.0, NEG).astype(np.float32)
    c["cmask"] = cm
    key = np.arange(T)
    alk = np.stack([np.ones(T), np.ones(T), 64.0 * (key // 64), 64.0 * (key // 64), key % 64, key % 64, 64.0 * (key // 64), key % 64]).astype(np.float32)
    E = (key[None, :] // 64 == np.arange(32)[:, None]).astype(np.float32)
    c["kaug_s"] = np.concatenate([E, alk], 0)
    c["kaug_w"] = np.concatenate([np.zeros((32, T), np.float32), alk], 0)
    pc = cend[:NCMP]
    alc = np.stack([np.ones(NCMP), np.ones(NCMP), 64.0 * (pc // 64), 64.0 * (pc // 64), pc % 64, pc % 64, 64.0 * (pc // 64), pc % 64]).astype(np.float32)
    kc = np.concatenate([np.zeros((32, NCMP), np.float32), alc], 0)
    c["kaug_c"] = np.tile(kc, (1, NG))
    slopes = np.exp2(-8.0 * np.arange(1, NH + 1, dtype=np.float64) / NH).astype(np.float32)
    qal = np.zeros((NH, 8, T), np.float32)
    for h in range(NH):
        a = (-slopes[h].astype(np.float64) * tt).astype(np.float32)
        ahi = _bf16r(a)
        alo = _bf16r(a - ahi)
        shi = _bf16r(slopes[h:h + 1])[0]
        slo = _bf16r(np.float32(slopes[h] - shi).reshape(1))[0]
        slo2 = _bf16r(np.float32(np.float64(slopes[h]) - np.float64(shi) - np.float64(slo)).reshape(1))[0]
        qal[h, 0], qal[h, 1] = ahi, alo
        qal[h, 2], qal[h, 3], qal[h, 4], qal[h, 5] = shi, slo, shi, slo
        qal[h, 6], qal[h, 7] = slo2, slo2
    c["qal"] = qal.astype(ml_dtypes.bfloat16)
    cs = 16 * np.arange(NCMP)
    ce = cs + 31
    ss = 64 * np.arange(32)
    ov = ((cs[:, None] <= ss[None, :] + 63) & (ce[:, None] >= ss[None, :])).astype(np.float32)
    vcc = np.concatenate([np.ones((NCMP, 1), np.float32), ov], 1)
    c["vcc"] = np.tile(vcc[:, None, :], (1, NG, 1))
    cur = (tt // 64)[:, None]
    j = np.arange(32)[None, :]
    forced = (j == 0) | (j == cur) | (j == cur - 1)
    F = np.where(forced, 100.0, np.where(j <= cur, 0.0, -100.0)).astype(np.float32)
    c["fimp"] = np.ascontiguousarray(F.reshape(16, 128, 32).transpose(1, 0, 2))
    return c


_CONST_SHAPES = {"ident": [128, 128], "tri_up": [128, 128], "tri_lo": [128, 128], "cmask": [128, T],
                 "kaug_s": [40, T], "kaug_w": [40, T], "kaug_c": [40, NG * NCMP], "qal": [NH, 8, T],
                 "vcc": [NCMP, NG, 33], "fimp": [128, 16, 32]}

_IN_SHAPES = {"x": [T, D], "w_in": [D, N_IN], "nw_b": [128, D], "conv_wT": [128, 8, 3], "conv_bT": [128, 8],
              "peT_k": [64, 64], "peT_v": [64, 64], "w1k": [2048, 128], "w1v": [2048, 128],
              "w2k": [128, 64], "w2v": [128, 64], "wpa": [D, D], "wpb": [D, D], "wo": [D, D], "fnw_b": [128, D]}


def build_nc(debug=False, stages=99):
    nc = bass.Bass("TRN2", target_bir_lowering=False)
    dr = {}
    for n, s in list(_IN_SHAPES.items()) + list(_CONST_SHAPES.items()):
        dr[n] = nc.dram_tensor(n, s, BF16 if n == "qal" else F32, kind="ExternalInput").ap()
    out_d = nc.dram_tensor("out", [T, D], F32, kind="ExternalOutput").ap()
    dbg = {}
    S = Sched(nc)
    B = S.buf
    sb = lambda n, shp, dt: nc.alloc_sbuf_tensor('s_' + n, shp, dt)
    PS = [nc.alloc_psum_tensor("psb%d" % i, [128, 512], F32) for i in range(8)]
    PB = ["ps%d" % i for i in range(8)]

    def psbf(i):
        return PS[i][:, :].bitcast(BF16)

    hT = sb("hT", [128, 8, T], BF16)
    obT = sb("obT", [128, 8, T], BF16)
    NWB = 8
    wpool = [sb("wblk%d" % i, [128, 8, 256], BF16) for i in range(NWB)]
    ident = sb("ident", [128, 128], BF16)
    tri_up = sb("tri_up", [128, 128], BF16)
    tri_lo = sb("tri_lo", [128, 128], BF16)
    gates = sb("gates", [128, 16, 48], F32)
    KC = sb("KC", [128, NG * NCMP], BF16)
    VC = sb("VC", [NCMP, NG, 97], BF16)
    cwT = sb("cwT", [128, 8, 3], F32)
    cbT = sb("cbT", [128, 8], F32)
    stat = sb("stat", [128, 64], F32)

    win_v = dr["w_in"].rearrange("(kt p) c -> p kt c", p=128)

    def wv_(n):
        return dr[n].rearrange("(kt p) c -> p kt c", p=128)

    wplan = []
    wplan.append(([(O_GN, 48)], None))
    for g in range(NG):
        wplan.append(([(O_KC + 64 * g, 64), (O_VC + 64 * g, 64)], None))
    for g in range(NG):
        wplan.append(([(O_Q + 256 * g, 256)], None))
        wplan.append(([(O_KS + 64 * g, 64), (O_KW + 64 * g, 64)], None))
        wplan.append(([(O_ZB + 256 * g, 256)], None))
        wplan.append(([(O_VS + 64 * g, 64), (O_VW + 64 * g, 64)], None))
    for cp in range(4):
        for off in (O_HA, O_B, O_C, O_ZA):
            wplan.append(([(off + 256 * cp, 256)], None))
    for cp in range(4):
        wplan.append(([(256 * cp, 256)], "wpa"))
        wplan.append(([(256 * cp, 256)], "wpb"))
        wplan.append(([(O_GM + 256 * cp, 256)], None))
        wplan.append(([(O_GM + D + 256 * cp, 256)], None))
    for j in range(4):
        wplan.append(([(256 * j, 256)], "wo"))
    wstate = {"next": 0, "issued": 0}
    AHEAD = 3

    def _wissue(upto):
        while wstate["issued"] < min(upto, len(wplan)):
            j = wstate["issued"]
            wstate["issued"] += 1
            segs, src = wplan[j]
            blk, bb = wpool[j % NWB], B("wblk%d" % (j % NWB))
            v = win_v if src is None else wv_(src)
            o = 0
            for (c0, n) in segs:
                S.dma("pool", blk[:, :, o:o + n], v[:, :, c0:c0 + n], writes=[bb])
                o += n

    def wprefetch(n):
        _wissue(wstate["next"] + n)

    def wblock(segs, src=None):
        j = wstate["next"]
        wstate["next"] += 1
        psegs, psrc = wplan[j]
        assert psegs == list(segs) and (psrc == src), (j, psegs, segs, psrc, src)
        _wissue(j + 1 + AHEAD)
        return wpool[j % NWB], B("wblk%d" % (j % NWB))
    wblock.prefetch = wprefetch

    def cload(dst, src, name, cast):
        S.dma("pool" if cast else "sp", dst, src, writes=[B(name)])

    cload(ident[:, :], dr["ident"], "ident", True)
    cload(tri_up[:, :], dr["tri_up"], "tri_up", True)
    cload(tri_lo[:, :], dr["tri_lo"], "tri_lo", True)
    cload(cwT[:, :, :], dr["conv_wT"], "cwT", False)
    cload(cbT[:, :], dr["conv_bT"], "cbT", False)

    def proj_fm(ps_ap, psname, blk, bb, c0, m, tg, extra_reads=()):
        fns = []
        for k in range(8):
            fns.append(lambda e, k=k: e.matmul(ps_ap, blk[:, k, c0:c0 + m], hT[:, k, tg * 512:(tg + 1) * 512],
                                               start=(k == 0), stop=(k == 7)))
        S.mm(fns, reads=[bb, B("hT%d" % tg)] + list(extra_reads), writes=[B(psname)])

    xv = dr["x"].rearrange("(tt p) d -> tt p d", p=128)
    with ExitStack() as _es:
        xts = [_es.enter_context(nc.sbuf_tensor("s_xt%d" % i, [128, D], F32)) for i in range(4)]
        xns = [_es.enter_context(nc.sbuf_tensor("s_xn%d" % i, [128, D], BF16)) for i in range(4)]
        nwb = _es.enter_context(nc.sbuf_tensor("s_nwb", [128, D], F32))
        junk = _es.enter_context(nc.sbuf_tensor("s_junk", [128, D], F32))
        st0 = _es.enter_context(nc.sbuf_tensor("s_st0", [128, 48], F32))
        cload(nwb[:, :], dr["nw_b"], "nwb", False)
        def s0_a(tt):
            xt, xn = xts[tt % 4], xns[tt % 4]
            bx, bn = B("xt%d" % (tt % 4)), B("xn%d" % (tt % 4))
            S.dma("sp", xt[:, :], xv[tt], writes=[bx])
            S.op("act", lambda e: e.activation(junk[:, :], xt[:, :], AF.Square, accum_out=st0[:, tt:tt + 1]),
                 reads=[bx], writes=[B("junk"), B("ss%d" % tt)])
            S.op("act", lambda e: e.activation(st0[:, 16 + tt:17 + tt], st0[:, tt:tt + 1], AF.Sqrt, bias=EPS, scale=1.0 / D),
                 reads=[B("ss%d" % tt)], writes=[B("sd%d" % tt)])
            S.op("dve", lambda e: e.reciprocal(st0[:, 32 + tt:33 + tt], st0[:, 16 + tt:17 + tt]),
                 reads=[B("sd%d" % tt)], writes=[B("rs%d" % tt)])
            S.op("dve", lambda e: e.scalar_tensor_tensor(out=xn[:, :], in0=xt[:, :], scalar=st0[:, 32 + tt:33 + tt],
                                                         in1=nwb[:, :], op0=ALU.mult, op1=ALU.mult),
                 reads=[bx, B("rs%d" % tt), B("nwb")], writes=[bn])

        def s0_b(tt):
            xn, bn = xns[tt % 4], B("xn%d" % (tt % 4))
            pb = tt % 4
            pv = psbf(pb)
            S.mm([lambda e, k=k: e.transpose(pv[:, k * 128:(k + 1) * 128], xn[:, k * 128:(k + 1) * 128], ident[:, :]) for k in range(8)],
                 reads=[bn, B("ident")], writes=[B(PB[pb])])
            S.op("act", lambda e: e.copy(hT[:, :, tt * 128:(tt + 1) * 128], pv.rearrange("p (k t) -> p k t", k=8)),
                 reads=[B(PB[pb])], writes=[B("hT%d" % (tt // 4))])

        s0_a(0)
        s0_a(1)
        for tt in range(16):
            if tt + 2 < 16:
                s0_a(tt + 2)
            s0_b(tt)
        S.barrier()
    if stages <= 0:
        _dbg_out(nc, S, dbg, debug, hT=hT)
        S.finish()
        return nc

    return _build_rest(nc, S, dr, out_d, dbg, debug, stages, PS, PB, psbf, hT, obT, wblock, proj_fm, ident, tri_up, tri_lo,
                       gates, KC, VC, stat, cload, cwT, cbT)


def _conv(nc, S, PS, PB, hT, out_aT, wblock, proj_fm, cwT, cbT):
    B = S.buf
    with ExitStack() as _es:
        u = _es.enter_context(nc.sbuf_tensor("s_u", [128, 2 + T], F32))
        has0 = _es.enter_context(nc.sbuf_tensor("s_has0", [128, 512], F32))
        has1 = _es.enter_context(nc.sbuf_tensor("s_has1", [128, 512], F32))
        sz0 = _es.enter_context(nc.sbuf_tensor("s_sz0", [128, 512], F32))
        sz1 = _es.enter_context(nc.sbuf_tensor("s_sz1", [128, 512], F32))
        y0 = _es.enter_context(nc.sbuf_tensor("s_y0", [128, 512], F32))
        y1 = _es.enter_context(nc.sbuf_tensor("s_y1", [128, 512], F32))
        S.op("dve", lambda e: e.memset(u[:, 0:2], 0.0), writes=[B("upad")])
        it = 0
        for cp in range(4):
            blks = [wblock([(off + 256 * cp, 256)]) for off in (O_HA, O_B, O_C, O_ZA)]
            for cl in range(2):
                ct = 2 * cp + cl
                for tg in range(4):
                    par = it % 2
                    it += 1
                    pbase = 4 * par
                    for j in range(4):
                        proj_fm(PS[pbase + j][:, :], PB[pbase + j], blks[j][0], blks[j][1], cl * 128, 128, tg)
                    has, sz, y = [has0, has1][par], [sz0, sz1][par], [y0, y1][par]
                    bh, bs, by = B("has%d" % par), B("sz%d" % par), B("y%d" % par)
                    S.op("act", lambda e, has=has, pbase=pbase: e.copy(has[:, :], PS[pbase][:, :]), reads=[B(PB[pbase])], writes=[bh])
                    S.op("act", lambda e, sz=sz, pbase=pbase: e.activation(sz[:, :], PS[pbase + 3][:, :], AF.Silu), reads=[B(PB[pbase + 3])], writes=[bs])
                    o = 2 + tg * 512
                    ub = [B("u%d" % tg)]
                    ur = [B("u%d" % tg), B("upad")] + ([B("u%d" % (tg - 1))] if tg else [])
                    S.op("dve", lambda e, has=has, pbase=pbase, o=o: e.tensor_tensor(u[:, o:o + 512], PS[pbase + 2][:, :], has[:, :], op=ALU.mult),
                         reads=[B(PB[pbase + 2]), bh], writes=ub)
                    S.op("dve", lambda e, y=y, o=o, ct=ct: e.tensor_scalar(y[:, :], u[:, o:o + 512], cwT[:, ct, 2:3], cbT[:, ct:ct + 1], op0=ALU.mult, op1=ALU.add),
                         reads=ur + [B("cwT"), B("cbT")], writes=[by])
                    S.op("dve", lambda e, y=y, o=o, ct=ct: e.scalar_tensor_tensor(out=y[:, :], in0=u[:, o - 1:o + 511], scalar=cwT[:, ct, 1:2], in1=y[:, :], op0=ALU.mult, op1=ALU.add),
                         reads=ur + [by], writes=[by])
                    S.op("dve", lambda e, y=y, o=o, ct=ct: e.scalar_tensor_tensor(out=y[:, :], in0=u[:, o - 2:o + 510], scalar=cwT[:, ct, 0:1], in1=y[:, :], op0=ALU.mult, op1=ALU.add),
                         reads=ur + [by], writes=[by])
                    S.op("dve", lambda e, y=y, pbase=pbase: e.tensor_tensor(y[:, :], PS[pbase + 1][:, :], y[:, :], op=ALU.mult),
                         reads=[B(PB[pbase + 1]), by], writes=[by])
                    S.op("dve", lambda e, y=y, sz=sz, ct=ct, tg=tg: e.tensor_tensor(out_aT[:, ct, tg * 512:(tg + 1) * 512], y[:, :], sz[:, :], op=ALU.mult),
                         reads=[by, bs], writes=[B("out_aT")])
        S.barrier()


def _dbg_out(nc, S, dbg, debug, **tiles):
    if not debug:
        return
    for n, t in tiles.items():
        shp = list(t.shape)
        d = nc.dram_tensor("dbg_" + n, shp, t.dtype, kind="ExternalOutput").ap()
        S.barrier()
        S.dma("sp", d, t[tuple(slice(None) for _ in shp)], is_out=True)


def _build_rest(nc, S, dr, out_d, dbg, debug, stages, PS, PB, psbf, hT, obT, wblock, proj_fm, ident, tri_up, tri_lo,
                gates, KC, VC, stat, cload, cwT, cbT):
    B = S.buf

    def hT_tt(k, tt):
        return hT[:, k, tt * 128:(tt + 1) * 128]

    gblk, gbb = wblock([(O_GN, 48)])
    for tt in range(16):
        pb = tt % 2
        S.mm([lambda e, k=k, tt=tt, pb=pb: e.matmul(PS[pb][:, 0:48], hT_tt(k, tt), gblk[:, k, 0:48], start=(k == 0), stop=(k == 7)) for k in range(8)],
             reads=[gbb, B("hT%d" % (tt // 4))], writes=[B(PB[pb])])
        S.op("act", lambda e, tt=tt, pb=pb: e.activation(gates[:, tt, :], PS[pb][:, 0:48], AF.Sigmoid), reads=[B(PB[pb])], writes=[B("gates")])

    S.op("dve", lambda e: e.memset(KC[96:128, :], 0.0), writes=[B("KC")])
    cload(KC[64:104, :], dr["kaug_c"], "KC", True)
    cload(VC[:, :, 64:97], dr["vcc"], "VC", True)
    with ExitStack() as _es:
        w1k = _es.enter_context(nc.sbuf_tensor("s_w1k", [64, 32, 128], BF16))
        w1v = _es.enter_context(nc.sbuf_tensor("s_w1v", [64, 32, 128], BF16))
        w2k = _es.enter_context(nc.sbuf_tensor("s_w2k", [128, 64], BF16))
        w2v = _es.enter_context(nc.sbuf_tensor("s_w2v", [128, 64], BF16))
        pek = _es.enter_context(nc.sbuf_tensor("s_pek", [64, 64], BF16))
        pev = _es.enter_context(nc.sbuf_tensor("s_pev", [64, 64], BF16))
        kcT = _es.enter_context(nc.sbuf_tensor("s_kcT", [64, 16, 128], BF16))
        vcT = _es.enter_context(nc.sbuf_tensor("s_vcT", [64, 16, 128], BF16))
        kcT1 = _es.enter_context(nc.sbuf_tensor("s_kcT1", [64, 16, 128], BF16))
        kcT2 = _es.enter_context(nc.sbuf_tensor("s_kcT2", [64, 16, 128], BF16))
        kcT3 = _es.enter_context(nc.sbuf_tensor("s_kcT3", [64, 16, 128], BF16))
        vcT1 = _es.enter_context(nc.sbuf_tensor("s_vcT1", [64, 16, 128], BF16))
        vcT2 = _es.enter_context(nc.sbuf_tensor("s_vcT2", [64, 16, 128], BF16))
        vcT3 = _es.enter_context(nc.sbuf_tensor("s_vcT3", [64, 16, 128], BF16))
        hk1 = _es.enter_context(nc.sbuf_tensor("s_hk1", [128, 128], BF16))
        hv1 = _es.enter_context(nc.sbuf_tensor("s_hv1", [128, 128], BF16))
        hk = _es.enter_context(nc.sbuf_tensor("s_hk", [128, 128], BF16))
        hv = _es.enter_context(nc.sbuf_tensor("s_hv", [128, 128], BF16))
        cload(w1k[:, :, :], dr["w1k"].rearrange("(p d) h -> d p h", d=64), "w1k", True)
        cload(w1v[:, :, :], dr["w1v"].rearrange("(p d) h -> d p h", d=64), "w1v", True)
        cload(w2k[:, :], dr["w2k"], "w2k", True)
        cload(w2v[:, :], dr["w2v"], "w2v", True)
        cload(pek[:, :], dr["peT_k"], "pek", True)
        cload(pev[:, :], dr["peT_v"], "pev", True)
        kcTs, vcTs = [kcT, kcT1, kcT2, kcT3], [vcT, vcT1, vcT2, vcT3]
        hks, hvs = [hk, hk1], [hv, hv1]
        for g in range(NG):
            blk, bb = wblock([(O_KC + 64 * g, 64), (O_VC + 64 * g, 64)])
            for (dst, c0_, nm) in ((kcTs[g], 0, "kcT%d" % g), (vcTs[g], 64, "vcT%d" % g)):
                for tg in range(4):
                    pb = tg
                    proj_fm(PS[pb][0:64, :], PB[pb], blk, bb, c0_, 64, tg)
                    S.op("act" if tg % 2 else "dve",
                         (lambda e, dst=dst, tg=tg, pb=pb: e.copy(dst[:, :, tg * 32:(tg + 1) * 32], PS[pb][0:64, :].rearrange("d (c p) -> d p c", p=16))) if tg % 2 else
                         (lambda e, dst=dst, tg=tg, pb=pb: e.tensor_copy(dst[:, :, tg * 32:(tg + 1) * 32], PS[pb][0:64, :].rearrange("d (c p) -> d p c", p=16))),
                         reads=[B(PB[pb])], writes=[B(nm)])
        for (w1, pe, col, nm) in ((w1k, pek, 48, "k"), (w1v, pev, 49, "v")):
            S.mm([lambda e, p=p, w1=w1, pe=pe: e.matmul(PS[0][:, 0:2], w1[:, p, :], pe[:, 2 * p:2 * p + 2], start=(p == 0), stop=(p == 31)) for p in range(32)],
                 reads=[B("w1" + nm), B("pe" + nm)], writes=[B(PB[0])])
            S.op("act", lambda e, col=col: e.copy(stat[:, col:col + 1], PS[0][:, 0:1]), reads=[B(PB[0])], writes=[B("cb" + nm)])
        for g in range(NG):
            par = g % 2
            for (src, w1, hid, col, nm, pbk) in ((kcTs[g], w1k, hks[par], 48, "k", 4 + 2 * par), (vcTs[g], w1v, hvs[par], 49, "v", 5 + 2 * par)):
                S.mm([lambda e, p=p, w1=w1, src=src, pbk=pbk: e.matmul(PS[pbk][:, 0:NCMP], w1[:, p, :], src[:, p % 16, (p // 16):(p // 16) + NCMP], start=(p == 0), stop=(p == 31)) for p in range(32)],
                     reads=[B("w1" + nm), B("%scT%d" % (nm, g))], writes=[B(PB[pbk])])
                S.op("act", lambda e, hid=hid, col=col, pbk=pbk: e.activation(hid[:, 0:NCMP], PS[pbk][:, 0:NCMP], AF.Silu, bias=stat[:, col:col + 1]),
                     reads=[B(PB[pbk]), B("cb" + nm)], writes=[B("hid%s%d" % (nm, par))])
            S.mm([lambda e: e.matmul(PS[0 + 2 * par][0:64, 0:NCMP], w2k[:, :], hks[par][:, 0:NCMP], start=True, stop=True)],
                 reads=[B("w2k"), B("hidk%d" % par)], writes=[B(PB[0 + 2 * par])])
            S.op("dve", lambda e: e.tensor_copy(KC[0:64, g * NCMP:(g + 1) * NCMP], PS[0 + 2 * par][0:64, 0:NCMP]), reads=[B(PB[0 + 2 * par])], writes=[B("KC")])
            S.mm([lambda e: e.matmul(PS[1 + 2 * par][0:NCMP, 0:64], hvs[par][:, 0:NCMP], w2v[:, :], start=True, stop=True)],
                 reads=[B("w2v"), B("hidv%d" % par)], writes=[B(PB[1 + 2 * par])])
            S.op("dve", lambda e: e.tensor_copy(VC[:, g, 0:64], PS[1 + 2 * par][0:NCMP, 0:64]), reads=[B(PB[1 + 2 * par])], writes=[B("VC")])
        wblock.prefetch(4)
        S.barrier()
    if stages <= 1:
        _dbg_out(nc, S, dbg, debug, hT=hT, gates=gates, KC=KC, VC=VC)
        S.finish()
        return nc
    _attention(nc, S, dr, PS, PB, psbf, hT, obT, wblock, proj_fm, ident, tri_up, tri_lo, gates, KC, VC, cload, hT_tt)
    if stages <= 2:
        _dbg_out(nc, S, dbg, debug, obT=obT)
        S.finish()
        return nc
    with nc.sbuf_tensor("s_out_aT", [128, 8, T], BF16) as out_aT:
        _conv(nc, S, PS, PB, hT, out_aT, wblock, proj_fm, cwT, cbT)
        if stages <= 3:
            _dbg_out(nc, S, dbg, debug, obT=obT, out_aT=out_aT)
            S.finish()
            return nc
        _tail(nc, S, dr, out_d, PS, PB, hT, out_aT, obT, wblock, proj_fm, cload)
    S.finish()
    return nc


def _attention(nc, S, dr, PS, PB, psbf, hT, obT, wblock, proj_fm, ident, tri_up, tri_lo, gates, KC, VC, cload, hT_tt):
    B = S.buf
    T_ = lambda n, shp, dt: nc.sbuf_tensor('s_' + n, shp, dt)
    with ExitStack() as _es:
        Qs0 = _es.enter_context(T_("Qs0", [128, T], BF16))
        Qs1 = _es.enter_context(T_("Qs1", [128, T], BF16))
        Qs2 = _es.enter_context(T_("Qs2", [128, T], BF16))
        Qs3 = _es.enter_context(T_("Qs3", [128, T], BF16))
        KS = _es.enter_context(T_("KS", [128, T], BF16))
        KW = _es.enter_context(T_("KW", [128, T], BF16))
        V2 = _es.enter_context(T_("V2", [128, 16, 2, 65], BF16))
        zsT = _es.enter_context(T_("zsT", [128, 2, T], F32))
        O_st0 = _es.enter_context(T_("O_st0", [128, 4, 4, 3, 65], F32))
        O_st1 = _es.enter_context(T_("O_st1", [128, 4, 4, 3, 65], F32))
        Pb0 = _es.enter_context(T_("Pb0", [128, 512], BF16))
        Pb1 = _es.enter_context(T_("Pb1", [128, 512], BF16))
        Pb2 = _es.enter_context(T_("Pb2", [128, 512], BF16))
        cmask = _es.enter_context(T_("cmask", [128, T], BF16))
        fimp = _es.enter_context(T_("fimp", [128, 16, 32], BF16))
        impacc = _es.enter_context(T_("impacc", [128, 4, 32], F32))
        impm = _es.enter_context(T_("impm", [128, 4, 32], F32))
        top8 = _es.enter_context(T_("top8", [128, 4, 8], F32))
        negw = _es.enter_context(T_("negw", [128, 4, 128], BF16))
        rsc = _es.enter_context(T_("rsc", [128, 8], F32))
        coefs = [_es.enter_context(T_("coef_%d" % i, [128, 4, 3], F32)) for i in range(4)]
        coef2s = [_es.enter_context(T_("coefb_%d" % i, [128, 4, 3], F32)) for i in range(4)]
        prod = _es.enter_context(T_("prod", [128, 4, 3, 64], F32))
        o4 = _es.enter_context(T_("o4", [128, 4, 64], F32))
        Pb3 = _es.enter_context(T_("Pb3", [128, 512], BF16))
        Pb4 = _es.enter_context(T_("Pb4", [128, 512], BF16))
        negw2 = _es.enter_context(T_("negw2", [128, 4, 128], BF16))
        cst0 = _es.enter_context(T_("cst0", [128, 4, 97], F32))
        cst1 = _es.enter_context(T_("cst1", [128, 4, 97], F32))
        obt0 = _es.enter_context(T_("obt0", [128, 256], BF16))
        obt1 = _es.enter_context(T_("obt1", [128, 256], BF16))
        obt2 = _es.enter_context(T_("obt2", [128, 256], BF16))
        obt3 = _es.enter_context(T_("obt3", [128, 256], BF16))
        Qs = [Qs0, Qs1, Qs2, Qs3]
        O_sts = [O_st0, O_st1]
        Pbs = [Pb0, Pb1, Pb2, Pb3, Pb4]
        negws = [negw, negw2]
        st = {"s": 0, "p": 0, "m": 0, "a": 0}
        pend = []
        deferred = []
        obts = [obt0, obt1, obt2, obt3]
        cst = [cst0, cst1]

        def flush(keep=0):
            while len(pend) > keep:
                ppv, ppost = pend.pop(0)
                pf, pr, pw = ppv()
                S.mm(pf, reads=pr, writes=pw)
                if ppost:
                    ppost()

        def nxt_s():
            st["s"] = (st["s"] + 1) % 3
            return st["s"]

        def nxt_p():
            st["p"] = (st["p"] + 1) % 5
            return st["p"]

        def nxt_m():
            st["m"] = (st["m"] + 1) % 3
            return 5 + st["m"]

        S.op("dve", lambda e: e.memset(KS[96:128, :], 0.0), writes=[B("KSc")])
        S.op("dve", lambda e: e.memset(KW[96:128, :], 0.0), writes=[B("KWc")])
        for hl in range(4):
            S.op("dve", lambda e, hl=hl: e.memset(Qs[hl][96:128, :], 0.0), writes=[B("Qal%d" % hl)])
        cload(KS[64:104, :], dr["kaug_s"], "KSc", True)
        cload(KW[64:104, :], dr["kaug_w"], "KWc", True)
        cload(cmask[:, :], dr["cmask"], "cmask", True)
        cload(fimp[:, :, :], dr["fimp"], "fimp", True)
        for hl in range(4):
            for tg in range(4):
                S.op("dve", lambda e, hl=hl, tg=tg: e.memset(Qs[hl][64:96, tg * 512:(tg + 1) * 512], 0.0), writes=[B("N%d_%d" % (hl, tg))])
        S.op("dve", lambda e: e.memset(V2[:, :, :, 64:65], 1.0), writes=[B("V2one")])
        S.op("dve", lambda e: e.memset(negw[:, :, :], 0.0), writes=[B("negw0")])
        S.op("dve", lambda e: e.memset(negw2[:, :, :], 0.0), writes=[B("negw1")])

        def qreads(hl, tg):
            return [B("Q%d_%d" % (hl, tg)), B("N%d_%d" % (hl, tg)), B("Qal%d" % hl)]

        for g in range(NG):
            bq, bqb = wblock([(O_Q + 256 * g, 256)])
            bk, bkb = wblock([(O_KS + 64 * g, 64), (O_KW + 64 * g, 64)])
            bz, bzb = wblock([(O_ZB + 256 * g, 256)])
            bv, bvb = wblock([(O_VS + 64 * g, 64), (O_VW + 64 * g, 64)])
            for hl in range(4):
                cload(Qs[hl][96:104, :], dr["qal"][4 * g + hl], "Qal%d" % hl, False)
            for j in range(2):
                for tg in range(4):
                    pb = nxt_m()
                    proj_fm(PS[pb][:, :], PB[pb], bq, bqb, j * 128, 128, tg)
                    S.op("act", lambda e: e.activation(Qs[2 * j][0:64, tg * 512:(tg + 1) * 512], PS[pb][0:64, :], AF.Copy, scale=0.125),
                         reads=[B(PB[pb])], writes=[B("Q%d_%d" % (2 * j, tg))])
                    S.op("act", lambda e: e.activation(Qs[2 * j + 1][0:64, tg * 512:(tg + 1) * 512], PS[pb][64:128, :], AF.Copy, scale=0.125),
                         reads=[B(PB[pb])], writes=[B("Q%d_%d" % (2 * j + 1, tg))])
            for tg in range(4):
                pb = nxt_m()
                proj_fm(PS[pb][:, :], PB[pb], bk, bkb, 0, 128, tg)
                S.op("dve", lambda e: e.tensor_copy(KS[0:64, tg * 512:(tg + 1) * 512], PS[pb][0:64, :]), reads=[B(PB[pb])], writes=[B("KS")])
                S.op("act", lambda e: e.copy(KW[0:64, tg * 512:(tg + 1) * 512], PS[pb][64:128, :]), reads=[B(PB[pb])], writes=[B("KW")])
            for tt in range(16):
                pb = nxt_m()
                S.mm([lambda e, k=k: e.matmul(PS[pb][:, 0:128], hT_tt(k, tt), bv[:, k, 0:128], start=(k == 0), stop=(k == 7)) for k in range(8)],
                     reads=[bvb, B("hT%d" % (tt // 4))], writes=[B(PB[pb])])
                S.op("dve", lambda e: e.tensor_copy(V2[:, tt, :, 0:64], PS[pb][:, 0:128].rearrange("p (b d) -> p b d", b=2)),
                     reads=[B(PB[pb])], writes=[B("V2")])
            for f in deferred:
                f()
            del deferred[:]
            for j in range(2):
                for tg in range(4):
                    pb = nxt_m()
                    proj_fm(PS[pb][:, :], PB[pb], bz, bzb, j * 128, 128, tg)
                    S.op("act", lambda e: e.activation(zsT[:, j, tg * 512:(tg + 1) * 512], PS[pb][:, :], AF.Silu), reads=[B(PB[pb])], writes=[B("zsil")])
            wblock.prefetch(4)
            def run_step(smm, exp, pv, pre=None, post=None):
                if pre:
                    pre()
                sf, sr, sw = smm()
                if len(pend) >= 3:
                    ppv, ppost = pend.pop(0)
                    pf, pr, pw = ppv()
                    S.mm(sf + pf, reads=sr + pr, writes=sw + pw)
                    exp()
                    if ppost:
                        ppost()
                else:
                    S.mm(sf, reads=sr, writes=sw)
                    exp()
                pend.append((pv, post))

            def unit(kind, hl, tg, pre_first=None, post_last=None):
                KA, kan, kcn, vb, bidx = (KS, "KS", "KSc", 0, 1) if kind == "s" else (KW, "KW", "KWc", 1, 2)
                kts = list(range(0, 4 * tg + 4)) if kind == "s" else list(range(max(0, 4 * tg - 4), 4 * tg + 4))
                st["a"] = (st["a"] + 1) % 2
                acc = 3 + st["a"]
                for idx, kt in enumerate(kts):
                    r = kt - 4 * tg
                    i_lo = max(0, r)
                    i_hi = 3 if kind == "s" else min(3, r + 4)
                    c0, c1 = i_lo * 128, (i_hi + 1) * 128
                    sbk = nxt_s()
                    pi = nxt_p()
                    Pb = Pbs[pi]

                    def smm(kt=kt, r=r, c0=c0, c1=c1, sbk=sbk):
                        specs = [(PS[sbk][:, c0:c1], KA[:, kt * 128:(kt + 1) * 128], Qs[hl][:, tg * 512 + c0:tg * 512 + c1])]
                        rd = [B(kan), B(kcn)] + qreads(hl, tg)
                        if r >= 0:
                            specs.append((PS[sbk][:, r * 128:(r + 1) * 128], ident[:, :], tri_up[:, :]))
                            rd += [B("ident"), B("tri_up")]
                        if kind == "w" and 0 <= r + 4 <= 3:
                            specs.append((PS[sbk][:, (r + 4) * 128:(r + 5) * 128], ident[:, :], tri_lo[:, :]))
                            rd += [B("ident"), B("tri_lo")]
                        n = len(specs)
                        return ([lambda e, sp=sp, j=j: e.matmul(sp[0], sp[1], sp[2], start=(j == 0), stop=(j == n - 1)) for j, sp in enumerate(specs)],
                                rd, [B(PB[sbk])])

                    def exp(c0=c0, c1=c1, sbk=sbk, pi=pi, Pb=Pb):
                        S.op("act", lambda e: e.activation(Pb[:, c0:c1], PS[sbk][:, c0:c1], AF.Exp), reads=[B(PB[sbk])], writes=[B("Pb%d" % pi)])

                    def pv(kt=kt, i_lo=i_lo, i_hi=i_hi, idx=idx, pi=pi, Pb=Pb):
                        fns = []
                        for i in range(i_lo, i_hi + 1):
                            fns.append(lambda e, i=i: e.matmul(PS[acc][:, i * 65:(i + 1) * 65], Pb[:, i * 128:(i + 1) * 128], V2[:, kt, vb, :],
                                                               start=(idx == 0 and i == i_lo), stop=(kt == 4 * tg + i), skip_group_check=True))
                        return (fns, [B("Pb%d" % pi), B("V2"), B("V2one")], [B(PB[acc])])

                    post = None
                    if idx == len(kts) - 1:
                        def post():
                            S.op("dve", lambda e: e.tensor_copy(O_sts[tg % 2][:, :, hl, bidx, :], PS[acc][:, 0:260].rearrange("p (i c) -> p i c", i=4)),
                                 reads=[B(PB[acc])], writes=[B("O_st%d" % (tg % 2))])
                            if post_last:
                                post_last()
                    run_step(smm, exp, pv, pre=pre_first if idx == 0 else None, post=post)

            def cmp_step(hl, tg, post_last=None):
                cols = slice(tg * 512, (tg + 1) * 512)
                sbk = nxt_s()
                pi = nxt_p()
                Pb = Pbs[pi]
                mb = nxt_m()
                mv = PS[mb][:, 0:388].rearrange("p (i c) -> p i c", i=4)

                def smm():
                    return ([lambda e: e.matmul(PS[sbk][0:NCMP, :], KC[:, g * NCMP:(g + 1) * NCMP], Qs[hl][:, cols], start=True, stop=False),
                             lambda e: e.matmul(PS[sbk][0:NCMP, :], ident[0:NCMP, 0:NCMP], cmask[0:NCMP, cols], start=False, stop=True)],
                            [B("KC"), B("ident"), B("cmask")] + qreads(hl, tg), [B(PB[sbk])])

                def exp():
                    S.op("act", lambda e: e.activation(Pb[0:NCMP, :], PS[sbk][0:NCMP, :], AF.Exp), reads=[B(PB[sbk])], writes=[B("Pb%d" % pi)])

                def pv():
                    return ([lambda e, i=i: e.matmul(PS[mb][:, i * 97:(i + 1) * 97], Pb[0:NCMP, i * 128:(i + 1) * 128], VC[0:NCMP, g, :], start=True, stop=True) for i in range(4)],
                            [B("Pb%d" % pi), B("VC")], [B(PB[mb])])

                def post():
                    cs = cst[hl % 2]
                    csn = "cst%d" % (hl % 2)
                    S.op("act", lambda e: e.copy(cs[:, :, :], mv[:, :, :]), reads=[B(PB[mb])], writes=[B(csn)])
                    S.op("dve", lambda e: e.tensor_copy(O_sts[tg % 2][:, :, hl, 0, :], cs[:, :, 0:65]), reads=[B(csn)], writes=[B("O_c%d" % (tg % 2))])
                    S.op("dve", lambda e: e.tensor_scalar(rsc[:, 0:4], cs[:, :, 64], 1e-30, None, op0=ALU.max), reads=[B(csn)], writes=[B("rsc")])
                    S.op("dve", lambda e: e.reciprocal(rsc[:, 4:8], rsc[:, 0:4]), reads=[B("rsc")], writes=[B("rsc2")])
                    if hl == 0:
                        S.op("dve", lambda e: e.tensor_tensor(impacc[:, :, :], cs[:, :, 65:97], rsc[:, 4:8].unsqueeze(2).to_broadcast([128, 4, 32]), op=ALU.mult),
                             reads=[B(csn), B("rsc2")], writes=[B("impacc")])
                    else:
                        S.op("dve", lambda e: e.tensor_tensor(cs[:, :, 65:97], cs[:, :, 65:97], rsc[:, 4:8].unsqueeze(2).to_broadcast([128, 4, 32]), op=ALU.mult),
                             reads=[B(csn), B("rsc2")], writes=[B(csn)])
                        S.op("dve", lambda e: e.tensor_tensor(impacc[:, :, :], impacc[:, :, :], cs[:, :, 65:97], op=ALU.add),
                             reads=[B(csn), B("impacc")], writes=[B("impacc")])
                    if post_last:
                        post_last()
                run_step(smm, exp, pv, post=post)

            for tg in range(4):
                cols = slice(tg * 512, (tg + 1) * 512)

                def selection(tg=tg):
                    S.op("dve", lambda e: e.tensor_tensor(impm[:, :, :], impacc[:, :, :], fimp[:, 4 * tg:4 * tg + 4, :], op=ALU.add),
                         reads=[B("impacc"), B("fimp")], writes=[B("impm")])
                    for i in range(4):
                        S.op("dve", lambda e, i=i: e.max(top8[:, i, :], impm[:, i, :]), reads=[B("impm")], writes=[B("top8")])
                    nw = negws[tg % 2]
                    for i in range(4):
                        S.op("dve", lambda e, i=i: e.tensor_scalar(nw[:, i, 64:96], impm[:, i, :], top8[:, i, 7:8], NEG, op0=ALU.is_lt, op1=ALU.mult),
                             reads=[B("impm"), B("top8")], writes=[B("negw%d" % (tg % 2))])

                sel_next = (lambda tg=tg: selection(tg + 1))

                def sel_to_q(tg=tg, cols=cols):
                    for f in deferred:
                        f()
                    del deferred[:]
                    mb = nxt_m()
                    pv_ = psbf(mb)
                    nw = negws[tg % 2]
                    S.mm([lambda e, i=i: e.transpose(pv_[:, i * 128:(i + 1) * 128], nw[:, i, :], ident[:, :]) for i in range(4)],
                         reads=[B("negw%d" % (tg % 2)), B("ident")], writes=[B(PB[mb])])
                    for hl in range(4):
                        S.op("dve", lambda e, hl=hl: e.tensor_copy(Qs[hl][64:96, cols], pv_[64:96, 0:512]), reads=[B(PB[mb])], writes=[B("N%d_%d" % (hl, tg))])

                def flush_deferred():
                    for f in deferred:
                        f()
                    del deferred[:]

                def normalize(tg=tg):
                    O_st = O_sts[tg % 2]
                    osn = "O_st%d" % (tg % 2)
                    ocn = "O_c%d" % (tg % 2)
                    for i in range(4):
                        tt = 4 * tg + i
                        cf, cf2 = coefs[i], coef2s[i]
                        S.op("dve", lambda e, i=i, cf=cf: e.tensor_scalar(cf[:, :, :], O_st[:, i, :, :, 64], 1e-30, None, op0=ALU.max), reads=[B(osn), B(ocn)], writes=[B("coef%d" % i)])
                        S.op("dve", lambda e, cf=cf, cf2=cf2: e.reciprocal(cf2[:, :, :], cf[:, :, :]), reads=[B("coef%d" % i)], writes=[B("coefb%d" % i)])
                        S.op("dve", lambda e, tt=tt, cf2=cf2: e.tensor_tensor(cf2[:, :, :], cf2[:, :, :], gates[:, tt, 12 * g:12 * g + 12].rearrange("p (h b) -> p h b", b=3), op=ALU.mult),
                             reads=[B("coefb%d" % i), B("gates")], writes=[B("coefb%d" % i)])
                    for i in range(4):
                        tt = 4 * tg + i
                        ob_i = obts[i]
                        cf2 = coef2s[i]
                        S.op("pool", lambda e, i=i, cf2=cf2: e.tensor_tensor(prod[:, :, :, :], O_st[:, i, :, :, 0:64], cf2[:, :, :].unsqueeze(3).to_broadcast([128, 4, 3, 64]), op=ALU.mult),
                             reads=[B(osn), B(ocn), B("coefb%d" % i)], writes=[B("prod")])
                        S.op("pool", lambda e: e.tensor_tensor(o4[:, :, :], prod[:, :, 0, :], prod[:, :, 1, :], op=ALU.add), reads=[B("prod")], writes=[B("o4")])
                        S.op("pool", lambda e, ob_i=ob_i: e.tensor_tensor(ob_i[:, :].rearrange("p (h d) -> p h d", h=4), o4[:, :, :], prod[:, :, 2, :], op=ALU.add),
                             reads=[B("prod"), B("o4")], writes=[B("obt%d" % i)])

                        def tr(tt=tt, i=i, ob_i=ob_i, g=g):
                            mb = nxt_m()
                            pv_ = psbf(mb)
                            S.mm([lambda e, j=j: e.transpose(pv_[:, j * 128:(j + 1) * 128], ob_i[:, j * 128:(j + 1) * 128], ident[:, :]) for j in range(2)],
                                 reads=[B("obt%d" % i), B("ident")], writes=[B(PB[mb])])
                            S.op("dve", lambda e: e.tensor_tensor(obT[:, 2 * g:2 * g + 2, tt * 128:(tt + 1) * 128], pv_[:, 0:256].rearrange("p (j t) -> p j t", j=2),
                                                                  zsT[:, :, tt * 128:(tt + 1) * 128], op=ALU.mult),
                                 reads=[B(PB[mb]), B("zsil")], writes=[B("obT")])
                        deferred.append(tr)

                def win_pre(tg=tg):
                    pass

                if tg == 0:
                    for hl in range(4):
                        cmp_step(hl, 0, post_last=selection if hl == 3 else None)
                for hl in range(4):
                    unit("w", hl, tg, pre_first=win_pre if hl == 0 else None)
                    if tg < 3:
                        cmp_step(hl, tg + 1, post_last=sel_next if hl == 3 else None)
                for hl in range(4):
                    unit("s", hl, tg, pre_first=sel_to_q if hl == 0 else None, post_last=normalize if hl == 3 else None)
            flush()
        for f in deferred:
            f()
        del deferred[:]
        S.barrier()


def _tail(nc, S, dr, out_d, PS, PB, hT, out_aT, obT, wblock, proj_fm, cload):
    B = S.buf
    T_ = lambda n, shp, dt: nc.sbuf_tensor('s_' + n, shp, dt)
    wv = lambda n: dr[n].rearrange("(kt p) c -> p kt c", p=128)
    with ExitStack() as _es:
        mixT = _es.enter_context(T_("mixT", [128, 8, T], BF16))
        sg0 = _es.enter_context(T_("sg0", [128, 512], F32))
        sg1 = _es.enter_context(T_("sg1", [128, 512], F32))
        sg2 = _es.enter_context(T_("sg2", [128, 512], F32))
        sg3 = _es.enter_context(T_("sg3", [128, 512], F32))
        m0 = _es.enter_context(T_("m0", [128, 512], F32))
        m1 = _es.enter_context(T_("m1", [128, 512], F32))
        xa = _es.enter_context(T_("xa", [128, D], F32))
        xb = _es.enter_context(T_("xb", [128, D], F32))
        xc = _es.enter_context(T_("xc", [128, D], F32))
        fnw = _es.enter_context(T_("fnw", [128, D], F32))
        junk2 = _es.enter_context(T_("junk2", [128, D], BF16))
        oa = _es.enter_context(T_("oa", [128, D], F32))
        ob_ = _es.enter_context(T_("ob_", [128, D], F32))
        st1 = _es.enter_context(T_("st1", [128, 48], F32))
        cload(fnw[:, :], dr["fnw_b"], "fnw", False)
        sgs, ms = [[sg0, sg1], [sg2, sg3]], [m0, m1]
        it = 0
        for cp in range(4):
            ba, bab = wblock([(256 * cp, 256)], src="wpa")
            bb_, bbb = wblock([(256 * cp, 256)], src="wpb")
            bg0, bg0b = wblock([(O_GM + 256 * cp, 256)])
            bg1, bg1b = wblock([(O_GM + D + 256 * cp, 256)])
            for cl in range(2):
                c = 2 * cp + cl
                for tg in range(4):
                    par = it % 2
                    it += 1
                    pb = 4 * par
                    tcols = slice(tg * 512, (tg + 1) * 512)
                    S.mm([lambda e, k=k: e.matmul(PS[pb][:, :], ba[:, k, cl * 128:(cl + 1) * 128], out_aT[:, k, tcols], start=(k == 0), stop=(k == 7)) for k in range(8)],
                         reads=[bab, B("out_aT")], writes=[B(PB[pb])])
                    S.mm([lambda e, k=k: e.matmul(PS[pb + 1][:, :], bb_[:, k, cl * 128:(cl + 1) * 128], obT[:, k, tcols], start=(k == 0), stop=(k == 7)) for k in range(8)],
                         reads=[bbb, B("obT")], writes=[B(PB[pb + 1])])
                    proj_fm(PS[pb + 2][:, :], PB[pb + 2], bg0, bg0b, cl * 128, 128, tg)
                    proj_fm(PS[pb + 3][:, :], PB[pb + 3], bg1, bg1b, cl * 128, 128, tg)
                    s0, s1, m = sgs[par][0], sgs[par][1], ms[par]
                    S.op("act", lambda e: e.activation(s0[:, :], PS[pb + 2][:, :], AF.Sigmoid), reads=[B(PB[pb + 2])], writes=[B("sga%d" % par)])
                    S.op("act", lambda e: e.activation(s1[:, :], PS[pb + 3][:, :], AF.Sigmoid), reads=[B(PB[pb + 3])], writes=[B("sgb%d" % par)])
                    S.op("dve", lambda e: e.tensor_tensor(m[:, :], PS[pb][:, :], s0[:, :], op=ALU.mult), reads=[B(PB[pb]), B("sga%d" % par)], writes=[B("m%d" % par)])
                    S.op("dve", lambda e: e.tensor_tensor(s1[:, :], PS[pb + 1][:, :], s1[:, :], op=ALU.mult), reads=[B(PB[pb + 1]), B("sgb%d" % par)], writes=[B("sgb%d" % par)])
                    S.op("dve", lambda e: e.tensor_tensor(mixT[:, c, tcols], m[:, :], s1[:, :], op=ALU.add), reads=[B("m%d" % par), B("sgb%d" % par)], writes=[B("mixT")])
        bo = [wblock([(256 * j, 256)], src="wo") for j in range(4)]
        xv = dr["x"].rearrange("(tt p) d -> tt p d", p=128)
        ov = out_d.rearrange("(tt p) d -> tt p d", p=128)
        os_ = [oa, ob_]
        xs = [xa, xb, xc]

        def t_load(tt):
            par = tt % 3
            S.dma("act", xs[par][:, :], xv[tt], writes=[B("xt%d" % par)])

        def t_a(tt):
            par = tt % 3
            xt = xs[par]
            pb = 2 * (tt % 4)
            fns = []
            for j in range(4):
                for k in range(8):
                    fns.append(lambda e, j=j, k=k: e.matmul(PS[pb + j // 2][:, (j % 2) * 256:(j % 2 + 1) * 256], mixT[:, k, tt * 128:(tt + 1) * 128], bo[j][0][:, k, :],
                                                            start=(k == 0), stop=(k == 7)))
            S.mm(fns, reads=[B("mixT")] + [b[1] for b in bo], writes=[B(PB[pb]), B(PB[pb + 1])])
            for hf in range(2):
                S.op("dve", lambda e, hf=hf: e.tensor_tensor(xt[:, hf * 512:(hf + 1) * 512], PS[pb + hf][:, :], xt[:, hf * 512:(hf + 1) * 512], op=ALU.add),
                     reads=[B(PB[pb + hf]), B("xt%d" % par)], writes=[B("xt%d" % par)])
            S.op("act", lambda e: e.activation(junk2[:, :], xt[:, :], AF.Square, accum_out=st1[:, tt:tt + 1]),
                 reads=[B("xt%d" % par)], writes=[B("junk2"), B("fs%d" % tt)])
            S.op("act", lambda e: e.activation(st1[:, 16 + tt:17 + tt], st1[:, tt:tt + 1], AF.Sqrt, bias=EPS, scale=1.0 / D), reads=[B("fs%d" % tt)], writes=[B("fd%d" % tt)])

        def t_b(tt):
            par = tt % 3
            xt, ot = xs[par], os_[tt % 2]
            S.op("dve", lambda e: e.reciprocal(st1[:, 32 + tt:33 + tt], st1[:, 16 + tt:17 + tt]), reads=[B("fd%d" % tt)], writes=[B("fr%d" % tt)])
            S.op("dve", lambda e: e.scalar_tensor_tensor(out=ot[:, :], in0=xt[:, :], scalar=st1[:, 32 + tt:33 + tt], in1=fnw[:, :], op0=ALU.mult, op1=ALU.mult),
                 reads=[B("xt%d" % par), B("fr%d" % tt), B("fnw")], writes=[B("ot%d" % (tt % 2))])
            S.dma("sp", ov[tt], ot[:, :], reads=[B("ot%d" % (tt % 2))], is_out=True)

        t_load(0)
        t_load(1)
        t_a(0)
        for tt in range(16):
            if tt + 2 < 16:
                t_load(tt + 2)
            if tt + 1 < 16:
                t_a(tt + 1)
            t_b(tt)
        S.barrier()


_NC_CACHE = {}


def _host_inputs(inputs):
    f = lambda a: np.ascontiguousarray(np.asarray(a, dtype=np.float32))
    l = 0
    shared = {
        "w_in": f(inputs["w_in"][l]),
        "nw_b": f(np.broadcast_to(np.asarray(inputs["norm_w"][l])[None, :], (128, D))),
        "conv_wT": f(np.asarray(inputs["conv_w"][l]).reshape(3, 8, 128).transpose(2, 1, 0)),
        "conv_bT": f(np.asarray(inputs["conv_b"][l]).reshape(8, 128).T),
        "peT_k": f(np.repeat(np.asarray(inputs["cmp_pe_k"][l]).T, 2, axis=1)),
        "peT_v": f(np.repeat(np.asarray(inputs["cmp_pe_v"][l]).T, 2, axis=1)),
        "w1k": f(inputs["cmp_w1_k"][l]), "w1v": f(inputs["cmp_w1_v"][l]),
        "w2k": f(inputs["cmp_w2_k"][l]), "w2v": f(inputs["cmp_w2_v"][l]),
        "wpa": f(inputs["w_proj_a"][l]), "wpb": f(inputs["w_proj_b"][l]), "wo": f(inputs["w_out"][l]),
        "fnw_b": f(np.broadcast_to(np.asarray(inputs["final_norm_w"])[None, :], (128, D))),
    }
    shared.update(_consts())
    return shared


def kernel(**inputs):
    key = "main"
    if key not in _NC_CACHE:
        _NC_CACHE[key] = build_nc()
    nc = _NC_CACHE[key]
    shared = _host_inputs(inputs)
    x = np.asarray(inputs["x"], dtype=np.float32)
    in_maps = []
    for b in range(8):
        m = dict(shared)
        m["x"] = np.ascontiguousarray(x[b])
        in_maps.append(m)
    res = run_bass_kernel_spmd(nc, in_maps, core_ids=list(range(8)))
    return np.stack([np.asarray(r["out"], dtype=np.float32) for r in res.results], axis=0)
```
